# Optimizing a Trainium2 kernel written in Bass

```python
import math
import jax, jax.numpy as jnp
from jax import lax
import numpy as np

D_MODEL = 1024
BATCH = 8
SEQ = 2048
DEPTH = 1
DEC_BATCH = 128
DEC_SEQ = 8
PAST_LEN = 16384
PAGE_SIZE = 128

S5_WIDTH = D_MODEL // 2
S5_GROUP = 16
S5_GROUPS = S5_WIDTH // S5_GROUP
S5_STATE = 64
DT_MIN = 1e-3
DT_MAX = 1e-1
POOL_WIDTH = D_MODEL // 2
POOL_WINDOWS = (2, 4, 8, 16)
POOL_GROUPS = len(POOL_WINDOWS)
POOL_GROUP_W = POOL_WIDTH // POOL_GROUPS
POOL_OUT_W = D_MODEL // POOL_GROUPS
POOL_BUF = max(POOL_WINDOWS) - 1
N_BRANCH = 2
IN_WIDTH = S5_WIDTH + POOL_WIDTH + N_BRANCH * D_MODEL
D_FF = 2816
EPS = 1e-6

kernel_name = "s5_pool_gated_macaron_decoder_step"


def rmsnorm(x, g):
    xf = x.astype(jnp.float32)
    y = xf * lax.rsqrt(jnp.mean(xf * xf, axis=-1, keepdims=True) + EPS) * g.astype(jnp.float32)
    return y.astype(x.dtype)


def swiglu(x, w_up, w_down):
    gu = jnp.matmul(x, w_up)
    g, u = jnp.split(gu, 2, axis=-1)
    return jnp.matmul(jax.nn.silu(g) * u, w_down)


def _complex_affine_combine(e1, e2):
    a1r, a1i, b1r, b1i = e1
    a2r, a2i, b2r, b2i = e2
    return (a2r * a1r - a2i * a1i,
            a2r * a1i + a2i * a1r,
            a2r * b1r - a2i * b1i + b2r,
            a2r * b1i + a2i * b1r + b2i)


def s5_branch(u, h0_re, h0_im, lam_re, lam_im, log_step, b_re, b_im, c_re, c_im, d_skip, w_glu):
    B, L, _ = u.shape
    uf = u.astype(jnp.float32).reshape(B, L, S5_GROUPS, S5_GROUP)
    lr = lam_re.astype(jnp.float32)
    li = lam_im.astype(jnp.float32)
    dt = jnp.exp(log_step.astype(jnp.float32))[:, None]
    mag = jnp.exp(lr * dt)
    ar = mag * jnp.cos(li * dt)
    ai = mag * jnp.sin(li * dt)
    den = lr * lr + li * li
    nr = ar - 1.0
    ni = ai
    cr = (nr * lr + ni * li) / den
    ci = (ni * lr - nr * li) / den
    br = b_re.astype(jnp.float32)
    bi = b_im.astype(jnp.float32)
    bbr = cr[..., None] * br - ci[..., None] * bi
    bbi = cr[..., None] * bi + ci[..., None] * br
    bu_r = jnp.einsum('blgh,gph->blgp', uf, bbr)
    bu_i = jnp.einsum('blgh,gph->blgp', uf, bbi)
    a_r = jnp.broadcast_to(ar, bu_r.shape)
    a_i = jnp.broadcast_to(ai, bu_i.shape)
    A_r, A_i, h_r, h_i = lax.associative_scan(_complex_affine_combine, (a_r, a_i, bu_r, bu_i), axis=1)
    h0r = h0_re.astype(jnp.float32)[:, None]
    h0i = h0_im.astype(jnp.float32)[:, None]
    h_r, h_i = (h_r + A_r * h0r - A_i * h0i,
                h_i + A_r * h0i + A_i * h0r)
    y = (jnp.einsum('gqp,blgp->blgq', c_re.astype(jnp.float32), h_r)
         - jnp.einsum('gqp,blgp->blgq', c_im.astype(jnp.float32), h_i)
         + d_skip.astype(jnp.float32).reshape(S5_GROUPS, S5_GROUP) * uf)
    y = jax.nn.gelu(y.reshape(B, L, S5_WIDTH))
    ab = jnp.matmul(y, w_glu.astype(jnp.float32))
    a, b = jnp.split(ab, 2, axis=-1)
    return a * jax.nn.sigmoid(b), h_r[:, -1], h_i[:, -1]


def pool_branch(v, buf, pos0, w_pool, pool_scale):
    B, L, C = v.shape
    vf = v.astype(jnp.float32)
    padded = jnp.concatenate([buf.astype(jnp.float32), vf], axis=1)
    cs = jnp.concatenate([jnp.zeros((B, 1, C), jnp.float32), jnp.cumsum(padded, axis=1)], axis=1)
    pos = pos0 + jnp.arange(L, dtype=jnp.float32)
    outs = []
    for g, w in enumerate(POOL_WINDOWS):
        sl = slice(g * POOL_GROUP_W, (g + 1) * POOL_GROUP_W)
        s = cs[:, POOL_BUF + 1:POOL_BUF + 1 + L, sl] - cs[:, POOL_BUF + 1 - w:POOL_BUF + 1 - w + L, sl]
        cnt = jnp.minimum(float(w), pos + 1.0)[None, :, None]
        outs.append(s / cnt - vf[..., sl])
    pooled = jnp.stack(outs, axis=2)
    y = jnp.einsum('blgc,gcd->blgd', pooled, w_pool.astype(jnp.float32)).reshape(B, L, D_MODEL)
    y = y * pool_scale.astype(jnp.float32)
    new_buf = padded[:, -POOL_BUF:].astype(buf.dtype)
    return y, new_buf


def block(x, h0_re, h0_im, pool_buf, pos0, ffn1_norm, ffn1_up, ffn1_down, mix_norm, w_in,
          lam_re, lam_im, log_step, b_re, b_im, c_re, c_im, d_skip, w_glu, w_pool, pool_scale,
          w_out, ffn2_norm, ffn2_up, ffn2_down):
    B, L, _ = x.shape
    x = x + (0.5 * swiglu(rmsnorm(x, ffn1_norm), ffn1_up, ffn1_down)).astype(x.dtype)
    hn = rmsnorm(x, mix_norm)
    z = jnp.matmul(hn, w_in)
    u_s = z[..., :S5_WIDTH]
    u_p = z[..., S5_WIDTH:S5_WIDTH + POOL_WIDTH]
    gates = jax.nn.sigmoid(z[..., S5_WIDTH + POOL_WIDTH:].astype(jnp.float32)).reshape(B, L, N_BRANCH, D_MODEL)
    y_s, h_re, h_im = s5_branch(u_s, h0_re, h0_im, lam_re, lam_im, log_step, b_re, b_im, c_re, c_im, d_skip, w_glu)
    y_p, new_buf = pool_branch(u_p, pool_buf, pos0, w_pool, pool_scale)
    merged = gates[:, :, 0] * y_s + gates[:, :, 1] * y_p
    x = x + jnp.matmul(merged, w_out.astype(jnp.float32)).astype(x.dtype)
    x = x + (0.5 * swiglu(rmsnorm(x, ffn2_norm), ffn2_up, ffn2_down)).astype(x.dtype)
    return x, h_re.astype(h0_re.dtype), h_im.astype(h0_im.dtype), new_buf


def setup_inputs(seed: int = 0) -> dict:
    key = jax.random.key(seed)
    ks = jax.random.split(key, 32)
    f32 = jnp.float32
    nrm = lambda k, shape, s: jax.random.normal(k, shape, f32) * s
    d = {}
    d["x_prompt"] = nrm(ks[0], (BATCH, SEQ, D_MODEL), 1.0)
    d["x_sample"] = nrm(ks[1], (DEC_BATCH, DEC_SEQ, D_MODEL), 1.0)
    d["state_ssm_re"] = nrm(ks[2], (DEPTH, DEC_BATCH, S5_GROUPS, S5_STATE), 0.5)
    d["state_ssm_im"] = nrm(ks[3], (DEPTH, DEC_BATCH, S5_GROUPS, S5_STATE), 0.5)
    d["state_pool"] = nrm(ks[4], (DEPTH, DEC_BATCH, POOL_BUF, POOL_WIDTH), 1.0)
    d["ffn1_norm"] = 1.0 + nrm(ks[5], (DEPTH, D_MODEL), 0.01)
    d["ffn1_up"] = nrm(ks[6], (DEPTH, D_MODEL, 2 * D_FF), D_MODEL ** -0.5)
    d["ffn1_down"] = nrm(ks[7], (DEPTH, D_FF, D_MODEL), D_FF ** -0.5)
    d["mix_norm"] = 1.0 + nrm(ks[8], (DEPTH, D_MODEL), 0.01)
    d["w_in"] = nrm(ks[9], (DEPTH, D_MODEL, IN_WIDTH), D_MODEL ** -0.5)
    d["lam_re"] = -0.5 + nrm(ks[10], (DEPTH, S5_GROUPS, S5_STATE), 0.01)
    d["lam_im"] = math.pi * jnp.arange(S5_STATE, dtype=f32) + nrm(ks[11], (DEPTH, S5_GROUPS, S5_STATE), 0.01)
    d["log_step"] = jax.random.uniform(ks[12], (DEPTH, S5_GROUPS), f32, math.log(DT_MIN), math.log(DT_MAX))
    d["b_re"] = nrm(ks[13], (DEPTH, S5_GROUPS, S5_STATE, S5_GROUP), (2 * S5_GROUP) ** -0.5)
    d["b_im"] = nrm(ks[14], (DEPTH, S5_GROUPS, S5_STATE, S5_GROUP), (2 * S5_GROUP) ** -0.5)
    d["c_re"] = nrm(ks[15], (DEPTH, S5_GROUPS, S5_GROUP, S5_STATE), (2 * S5_STATE) ** -0.5)
    d["c_im"] = nrm(ks[16], (DEPTH, S5_GROUPS, S5_GROUP, S5_STATE), (2 * S5_STATE) ** -0.5)
    d["d_skip"] = nrm(ks[17], (DEPTH, S5_WIDTH), 1.0)
    d["w_glu"] = nrm(ks[18], (DEPTH, S5_WIDTH, 2 * D_MODEL), S5_WIDTH ** -0.5)
    d["w_pool"] = nrm(ks[19], (DEPTH, POOL_GROUPS, POOL_GROUP_W, POOL_OUT_W), POOL_GROUP_W ** -0.5)
    d["pool_scale"] = 1.0 + nrm(ks[20], (DEPTH, D_MODEL), 0.1)
    d["w_out"] = nrm(ks[21], (DEPTH, D_MODEL, D_MODEL), D_MODEL ** -0.5)
    d["ffn2_norm"] = 1.0 + nrm(ks[22], (DEPTH, D_MODEL), 0.01)
    d["ffn2_up"] = nrm(ks[23], (DEPTH, D_MODEL, 2 * D_FF), D_MODEL ** -0.5)
    d["ffn2_down"] = nrm(ks[24], (DEPTH, D_FF, D_MODEL), D_FF ** -0.5)
    d["final_norm"] = 1.0 + nrm(ks[25], (D_MODEL,), 0.01)
    return d


def reference(x_prompt, x_sample, state_ssm_re, state_ssm_im, state_pool,
              ffn1_norm, ffn1_up, ffn1_down, mix_norm, w_in, lam_re, lam_im, log_step,
              b_re, b_im, c_re, c_im, d_skip, w_glu, w_pool, pool_scale, w_out,
              ffn2_norm, ffn2_up, ffn2_down, final_norm):
    xp = x_prompt
    xs = x_sample
    Bp = x_prompt.shape[0]
    zero_h = jnp.zeros((Bp, S5_GROUPS, S5_STATE), state_ssm_re.dtype)
    zero_buf = jnp.zeros((Bp, POOL_BUF, POOL_WIDTH), state_pool.dtype)
    p_re, p_im, p_buf, s_re, s_im, s_buf = [], [], [], [], [], []
    for l in range(DEPTH):
        w = (ffn1_norm[l], ffn1_up[l], ffn1_down[l], mix_norm[l], w_in[l], lam_re[l], lam_im[l],
             log_step[l], b_re[l], b_im[l], c_re[l], c_im[l], d_skip[l], w_glu[l], w_pool[l],
             pool_scale[l], w_out[l], ffn2_norm[l], ffn2_up[l], ffn2_down[l])
        xp, hr, hi, nb = block(xp, zero_h, zero_h, zero_buf, 0, *w)
        p_re.append(hr); p_im.append(hi); p_buf.append(nb)
        xs, hr, hi, nb = block(xs, state_ssm_re[l], state_ssm_im[l], state_pool[l], PAST_LEN, *w)
        s_re.append(hr); s_im.append(hi); s_buf.append(nb)
    y_prompt = rmsnorm(xp, final_norm)
    y_sample = rmsnorm(xs, final_norm)
    return (y_prompt, y_sample,
            jnp.stack(p_re), jnp.stack(p_im), jnp.stack(p_buf),
            jnp.stack(s_re), jnp.stack(s_im), jnp.stack(s_buf))
```

```python
import math
import os
from contextlib import ExitStack

import numpy as np
import concourse.bass as bass
import concourse.mybir as mybir
from concourse.bass_utils import run_bass_kernel_spmd

F32 = mybir.dt.float32
F32R = mybir.dt.float32r
I32 = mybir.dt.int32
AF = mybir.ActivationFunctionType
ALU = mybir.AluOpType

ENGS = ("pe", "act", "dve", "pool", "sp")


class Dep:
    __slots__ = ("name", "w", "r", "dsem", "dcnt", "swc")

    def __init__(self, name=""):
        self.name = name
        self.w = None
        self.r = []
        self.dsem = None
        self.dcnt = 0
        self.swc = None


class FW:
    def __init__(self, nc, same_engine_sync=False):
        self.nc = nc
        self.ops = {e: [] for e in ENGS}
        self.seq = {e: 0 for e in ENGS}
        self.same = same_engine_sync
        self.esem = {}
        self.waited = {e: {} for e in ENGS}

    def _collect(self, eng, toks):
        best = {}
        for tok in toks:
            if tok is None:
                continue
            if tok[0] == "eng":
                _, e2, s, hz = tok
                if e2 == eng and (eng == "pe" or not (self.same or hz)):
                    continue
                key = ("eng", e2)
                val = s
            else:
                key = ("dma", id(tok[1]))
                val = tok[2]
            if self.waited[eng].get(key, -1) >= val:
                continue
            if key not in best or best[key][0] < val:
                best[key] = (val, tok)
        waits = []
        for key, (val, tok) in best.items():
            self.waited[eng][key] = val
            waits.append(tok)
        return waits

    def op(self, eng, fn, reads=(), writes=(), dma=False, hz=False):
        cands = []
        for d in reads:
            cands.append(d.w)
        for d in writes:
            cands.append(d.w)
            cands.extend(d.r)
        waits = self._collect(eng, cands)
        seq = self.seq[eng]
        self.seq[eng] += 1
        rec = {"waits": waits, "fn": fn, "dma": None, "seq": seq}
        if dma:
            tgt = writes[0] if writes else reads[0]
            if eng == "pool":
                if tgt.swc is None:
                    tgt.swc = Dep(tgt.name + "_sw")
                tgt = tgt.swc
            tgt.dcnt += 16
            tok = ("dma", tgt, tgt.dcnt)
            rec["dma"] = tgt
        else:
            tok = ("eng", eng, seq, hz)
        for d in reads:
            d.r.append(tok)
        for d in writes:
            d.w = tok
            d.r = []
        self.ops[eng].append(rec)
        return tok

    def dma(self, eng, out, in_, reads=(), writes=(), **kw):
        return self.op(eng, lambda e: e.dma_start(out=out, in_=in_, **kw), reads, writes, dma=True)

    def finish_wait(self, eng, toks):
        waits = self._collect(eng, toks)
        self.ops[eng].append({"waits": waits, "fn": None, "dma": None, "seq": None})

    def emit(self):
        nc = self.nc
        needed = {e: set() for e in ENGS}
        ddeps = {}
        for e in ENGS:
            for rec in self.ops[e]:
                for t in rec["waits"]:
                    if t[0] == "eng":
                        needed[t[1]].add(t[2])
                    else:
                        ddeps[id(t[1])] = t[1]
                if rec["dma"] is not None:
                    ddeps[id(rec["dma"])] = rec["dma"]
        val_at = {}
        for e in ENGS:
            c = 0
            m = {}
            for s in range(self.seq[e]):
                if s in needed[e]:
                    c += 1
                m[s] = c
            val_at[e] = m
        for e in ENGS:
            self.esem[e] = nc.alloc_semaphore(name=f"sem_{e}")
        for i, d in enumerate(ddeps.values()):
            d.dsem = nc.alloc_semaphore(name=f"dsem_{i}")

        def run(e, h):
            for rec in self.ops[e]:
                for t in rec["waits"]:
                    if t[0] == "eng":
                        h.wait_ge(self.esem[t[1]], val_at[t[1]][t[2]])
                    else:
                        h.wait_ge(t[1].dsem, t[2])
                if rec["fn"] is None:
                    continue
                ins = rec["fn"](h)
                if rec["dma"] is not None:
                    ins.then_inc(rec["dma"].dsem, 16)
                elif rec["seq"] in needed[e]:
                    ins.then_inc(self.esem[e], 1)

        with nc.Block() as block:
            @block.tensor
            def _(h):
                run("pe", h)

            @block.scalar
            def _(h):
                run("act", h)

            @block.vector
            def _(h):
                run("dve", h)

            @block.gpsimd
            def _(h):
                run("pool", h)

            @block.sync
            def _(h):
                run("sp", h)


D = 1024
DFF = 2816
NK = 8
NF = 22
NT = 1088
C = 136
CP = 128
CS = 8
SUB = 272
NSUB = 4
EPS = 1e-6
NSLOT = 7

IN_SPECS = [
    ("xp", [2048, 1024]), ("xs", [16, 8, 1024]),
    ("st_re", [16, 32, 64]), ("st_im", [16, 32, 64]), ("st_pool", [16, 15, 512]),
    ("ffn1_norm", [1024]), ("ffn1_up", [1024, 5632]), ("ffn1_down", [2816, 1024]),
    ("mix_norm", [1024]), ("w_in", [1024, 3072]),
    ("lam_re", [32, 64]), ("lam_im", [32, 64]), ("log_step", [32]),
    ("b_re", [32, 64, 16]), ("b_im", [32, 64, 16]), ("c_re", [32, 16, 64]), ("c_im", [32, 16, 64]),
    ("d_skip", [512]), ("w_glu", [512, 2048]), ("w_pool", [4, 128, 256]), ("pool_scale", [1024]),
    ("w_out", [1024, 1024]), ("ffn2_norm", [1024]), ("ffn2_up", [1024, 5632]), ("ffn2_down", [2816, 1024]),
    ("final_norm", [1024]),
]
OUT_SPECS = [
    ("yp", [2048, 1024]), ("ys", [16, 8, 1024]),
    ("o_pre", [32, 64]), ("o_pim", [32, 64]), ("o_ppool", [15, 512]),
    ("o_sre", [16, 32, 64]), ("o_sim", [16, 32, 64]), ("o_spool", [16, 15, 512]),
]

STAGE = {"mixer": False}


def build_program():
    nc = bass.Bass("TRN2", target_bir_lowering=False)
    I = {n: nc.dram_tensor(n, s, F32, kind="ExternalInput").ap() for n, s in IN_SPECS}
    O = {n: nc.dram_tensor(n, s, F32, kind="ExternalOutput").ap() for n, s in OUT_SPECS}
    fw = FW(nc, same_engine_sync=True)
    es = ExitStack()

    def sb(name, shape, dt=F32):
        return es.enter_context(nc.sbuf_tensor(name, shape, dt))

    xT = sb("xT", [128, NK, NT])
    hn = sb("hn", [128, NK, NT], F32R)
    R1 = sb("R1", [128, NK, NT], F32R)
    ring = sb("ring", [128, NSLOT, 1024], F32R)
    TMP = sb("TMP", [128, 5120])
    ident = sb("ident", [128, 128])
    ones = sb("ones", [128, 128], F32R)
    onesf = sb("onesf", [128, 128])
    gains = sb("gains", [128, 4, NK])
    small = sb("small", [128, 128])
    sqt = sb("sqt", [128, 2, SUB], F32R)
    ps = es.enter_context(nc.psum_tensor("ps", [128, 8, 512], F32))

    d_xT = [Dep(f"xT{k}") for k in range(NK)]
    d_hn = Dep("hn")
    d_R1 = [Dep(f"R1_{k}") for k in range(NK)]
    d_ring = [Dep(f"ring{i}") for i in range(NSLOT)]
    d_bank = [Dep(f"bank{i}") for i in range(8)]
    d_const = Dep("const")
    d_gain = Dep("gain")
    d_small = Dep("small")
    d_tmp = {}

    def tdep(name):
        if name not in d_tmp:
            d_tmp[name] = Dep(name)
        return d_tmp[name]

    state = {"bank": 0, "slot": 0, "flip": 0}

    def bank():
        b = state["bank"]
        state["bank"] = (b + 1) % 8
        return b, d_bank[b]

    def wload(src_ap, view_shape, reads=()):
        s = state["slot"]
        state["slot"] = (s + 1) % NSLOT
        n = 1
        for v in view_shape:
            n *= v
        dst = ring[:, s, 0:n]
        if len(view_shape) == 2:
            dst = dst.rearrange("p (a b) -> p a b", b=view_shape[1])
        fw.dma("pool", dst, src_ap.bitcast(F32R), reads=list(reads), writes=[d_ring[s]])
        return dst, d_ring[s]

    def mm(out, lhsT, rhs, start, stop, reads, writes):
        fw.op("pe", lambda e: e.matmul(out, lhsT, rhs, start=start, stop=stop), reads=reads, writes=writes)

    def tr(out, in_, idn, reads, writes):
        fw.op("pe", lambda e: e.transpose(out=out, in_=in_, identity=idn), reads=reads, writes=writes)

    def act(out, in_, func, reads, writes, **kw):
        fw.op("act", lambda e: e.activation(out=out, in_=in_, func=func, **kw), reads=reads, writes=writes)

    def tt(eng, out, in0, in1, op, reads, writes):
        fw.op(eng, lambda e: e.tensor_tensor(out=out, in0=in0, in1=in1, op=op), reads=reads, writes=writes)

    def stt(eng, out, in0, scalar, in1, op0, op1, reads, writes):
        fw.op(eng, lambda e: e.scalar_tensor_tensor(out=out, in0=in0, scalar=scalar, in1=in1, op0=op0, op1=op1),
              reads=reads, writes=writes)

    def ts(eng, out, in0, s1, s2, op0, op1, reads, writes):
        if op1 is None:
            fw.op(eng, lambda e: e.tensor_scalar(out=out, in0=in0, scalar1=s1, scalar2=None, op0=op0), reads=reads, writes=writes)
        else:
            fw.op(eng, lambda e: e.tensor_scalar(out=out, in0=in0, scalar1=s1, scalar2=s2, op0=op0, op1=op1),
                  reads=reads, writes=writes)

    def recip(out, in_, reads, writes):
        fw.op("dve", lambda e: e.reciprocal(out=out, in_=in_), reads=reads, writes=writes)

    def memset(eng, ap, val, writes):
        fw.op(eng, lambda e: e.memset(ap, val), writes=writes, hz=True)

    def alt():
        state["flip"] ^= 1
        return "act" if state["flip"] else "dve"

    def copy(eng, out, in_, reads, writes):
        if eng == "act":
            act(out, in_, AF.Copy, reads, writes)
        else:
            fw.op(eng, lambda e: e.tensor_copy(out=out, in_=in_), reads=reads, writes=writes)

    memset("pool", ident[:], 0.0, [d_const])
    fw.op("pool", lambda e: e.affine_select(out=ident[:], in_=ident[:], pattern=[[-1, 128]],
                                            compare_op=ALU.not_equal, fill=1.0, base=0, channel_multiplier=1),
          reads=[d_const], writes=[d_const], hz=True)
    memset("pool", onesf[:], 1.0, [d_const])
    act(ones[:], onesf[:], AF.Copy, [d_const], [d_const])
    for i, nm in enumerate(["ffn1_norm", "mix_norm", "ffn2_norm"]):
        fw.dma("sp", gains[:, i, :], I[nm].rearrange("(k p) -> p k", p=128), writes=[d_gain],
               allow_slow_non_contiguous=True)
    eps_t = small[:, 0:1]
    memset("pool", small[:, 0:1], EPS, [d_small])
    d_zpad = Dep("zpad")

    def zero_pads():
        memset("pool", TMP[64:128, 2048:3072], 0.0, [d_zpad])
        memset("pool", TMP[:, 3072:4096], 0.0, [d_zpad])

    xp_v = I["xp"].rearrange("(t c j) d -> t j c d", t=2, c=CP, j=8)
    yp_v = O["yp"].rearrange("(t c j) d -> t j c d", t=2, c=CP, j=8)

    def xsamp(k):
        return xT[:, k, :].rearrange("p (j c) -> p j c", c=C)[:, :, CP:C]

    def load_tile(t):
        ksub = os.environ.get("KSUB", "ps")
        for j in range(8 if "p" in ksub else 0):
            stg = TMP[:, (j % 2) * 1024:(j % 2 + 1) * 1024]
            dstg = tdep(f"stg{j % 2}")
            fw.dma("sp", stg, xp_v[t, j], writes=[dstg])
            for h in range(2):
                b, db = bank()
                for kk in range(4):
                    k = h * 4 + kk
                    tr(ps[:, b, kk * 128:(kk + 1) * 128], stg[:, k * 128:(k + 1) * 128], ident[:], [dstg, d_const], [db])
                copy(alt(), xT[:, h * 4:(h + 1) * 4, j * C:j * C + CP],
                     ps[:, b, :].rearrange("p (k c) -> p k c", c=128), [db], d_xT[h * 4:(h + 1) * 4])
        if "s" not in ksub:
            return
        stg = TMP[:, 2048:3072]
        sdeps = [tdep(f"stgs{j}") for j in range(8)]
        for j in range(8):
            fw.dma("sp", stg[j * 8:(j + 1) * 8, :], I["xs"][8 * t:8 * t + 8, j, :], reads=[d_zpad], writes=[sdeps[j]])
        kcuts = int(os.environ.get("KCUTS", "2"))
        for h in range(2 if kcuts >= 1 else 0):
            b, db = bank()
            for kk in range(4):
                k = h * 4 + kk
                tr(ps[:, b, kk * 128:(kk + 1) * 128], stg[:, k * 128:(k + 1) * 128], ident[:], sdeps + [d_const, d_zpad], [db])
            for kk in range(4 if kcuts >= 2 else 0):
                k = h * 4 + kk
                for j in range(8):
                    copy(alt(), xT[:, k, j * C + CP:(j + 1) * C], ps[:, b, kk * 128 + j * 8:kk * 128 + j * 8 + 8], [db], [d_xT[k]])

    def rmsnorm(gi):
        for s in range(NSUB):
            sl = slice(s * SUB, (s + 1) * SUB)
            b, db = bank()
            for k in range(NK):
                sq = sqt[:, k % 2, :]
                dsq = tdep(f"sq{k % 2}")
                act(sq, xT[:, k, sl], AF.Square, [d_xT[k]], [dsq])
                mm(ps[:, b, 0:SUB], ones[:], sq, k == 0, k == NK - 1, [dsq, d_const], [db])
            sd = TMP[:, 1024:1024 + SUB]
            dsd = tdep("stg1")
            act(sd, ps[:, b, 0:SUB], AF.Sqrt, [db, d_small], [dsd], scale=1.0 / D, bias=eps_t)
            recip(sd, sd, [dsd], [dsd])
            for k in range(NK):
                stt("dve", hn[:, k, sl], xT[:, k, sl], gains[:, gi, k:k + 1], sd, ALU.mult, ALU.mult,
                    [d_xT[k], dsd, d_gain], [d_hn])

    def ffn(gi, w_up, w_down):
        rmsnorm(gi)
        up_v = w_up.rearrange("(k p) n -> p k n", p=128)
        for (f0, f1) in [(0, 8), (8, 15), (15, 22)]:
            nf = f1 - f0
            for f in range(f0, f1):
                wg, dg = wload(up_v[:, :, f * 128:(f + 1) * 128], [NK, 128])
                wu, du = wload(up_v[:, :, DFF + f * 128:DFF + (f + 1) * 128], [NK, 128])
                for s in range(NSUB):
                    sl = slice(s * SUB, (s + 1) * SUB)
                    ba, dba = bank()
                    bb, dbb = bank()
                    for k in range(NK):
                        mm(ps[:, ba, 0:SUB], wg[:, k, :], hn[:, k, sl], k == 0, k == NK - 1, [dg, d_hn], [dba])
                    for k in range(NK):
                        mm(ps[:, bb, 0:SUB], wu[:, k, :], hn[:, k, sl], k == 0, k == NK - 1, [du, d_hn], [dbb])
                    sg = TMP[:, (s % 2) * SUB:(s % 2 + 1) * SUB]
                    dsg = tdep(f"sg{s % 2}")
                    act(sg, ps[:, ba, 0:SUB], AF.Silu, [dba], [dsg])
                    tt("dve", R1[:, f - f0, sl], sg, ps[:, bb, 0:SUB], ALU.mult, [dsg, dbb], [d_R1[f - f0]])
                    pump(PUMP)
            dn_v = w_down[f0 * 128:f1 * 128, :].rearrange("(f p) n -> p f n", p=128)
            for m in range(NK):
                wd, dd = wload(dn_v[:, :, m * 128:(m + 1) * 128], [nf, 128])
                for s in range(NSUB):
                    sl = slice(s * SUB, (s + 1) * SUB)
                    b, db = bank()
                    for i in range(nf):
                        mm(ps[:, b, 0:SUB], wd[:, i, :], R1[:, i, sl], i == 0, i == nf - 1, [dd, d_R1[i]], [db])
                    stt("dve", xT[:, m, sl], ps[:, b, 0:SUB], 0.5, xT[:, m, sl], ALU.mult, ALU.add, [db, d_xT[m]], [d_xT[m]])
                    pump(2 * PUMP)

    gfin = TMP[:, 4096:5120]
    d_gfin = Dep("gfin")
    out_toks = []
    d_out = Dep("out")

    def norm_rows(np_, banks, dbs, ot, dot, slot):
        col = 64 + 4 * slot
        dcol = tdep(f"fin{slot}")
        ssq = small[0:np_, col:col + 2]
        for h in range(2):
            act(ot[:, h * 512:(h + 1) * 512], ps[0:np_, banks[h], :], AF.Square, [dbs[h]], dot + [dcol],
                accum_out=ssq[:, h:h + 1])
        rs = small[0:np_, col + 2:col + 3]
        tt("dve", rs, ssq[:, 0:1], ssq[:, 1:2], ALU.add, [dcol], [dcol])
        act(rs, rs, AF.Sqrt, [dcol, d_small], [dcol], scale=1.0 / D, bias=eps_t[0:np_, :])
        recip(rs, rs, [dcol], [dcol])
        for h in range(2):
            stt("dve", ot[:, h * 512:(h + 1) * 512], ps[0:np_, banks[h], :], rs, gfin[0:np_, h * 512:(h + 1) * 512],
                ALU.mult, ALU.mult, [dbs[h], dcol, d_gfin], dot)

    def final_store(t, first):
        if first:
            fw.dma("sp", gfin, I["final_norm"].partition_broadcast(128), writes=[d_gfin])
        for j in range(8):
            banks, dbs = [], []
            for h in range(2):
                b, db = bank()
                banks.append(b)
                dbs.append(db)
                for kk in range(4):
                    k = h * 4 + kk
                    tr(ps[:, b, kk * 128:(kk + 1) * 128], xT[:, k, j * C:j * C + CP], ident[:], [d_xT[k], d_const], [db])
            ot = TMP[:, (j % 2) * 1024:(j % 2 + 1) * 1024]
            dot = tdep(f"stg{j % 2}")
            norm_rows(128, banks, dbs, ot, [dot], j)
            out_toks.append(fw.dma("sp", yp_v[t, j], ot, reads=[dot]))
        banks, dbs = [], []
        xsc = TMP[:, 3072:4096].rearrange("p (k n) -> p k n", n=128)
        dxsc = tdep("xsc")
        for k in range(NK):
            for j in range(8):
                copy("act", xsc[:, k, j * 8:(j + 1) * 8], xT[:, k, j * C + CP:(j + 1) * C], [d_xT[k], d_zpad], [dxsc])
        for h in range(2):
            b, db = bank()
            banks.append(b)
            dbs.append(db)
            for kk in range(4):
                k = h * 4 + kk
                tr(ps[:, b, kk * 128:(kk + 1) * 128], xsc[:, k, :], ident[:], [dxsc, d_const], [db])
        ot = TMP[0:64, 2048:3072]
        dots = [tdep(f"stgs{j}") for j in range(8)]
        norm_rows(64, banks, dbs, ot, dots, 8)
        for j in range(8):
            out_toks.append(fw.dma("sp", O["ys"][8 * t:8 * t + 8, j, :], ot[j * 8:(j + 1) * 8, :], reads=[dots[j]]))

    R2 = sb("R2", [128, NK, NT], F32)
    d_R2 = [Dep(f"R2_{k}") for k in range(NK)]
    xx = sb("xx", [128, 2, 2, 2, C], F32R)
    tabr = sb("tabr", [128, 2, 2, 2 * C])
    s5c = sb("s5c", [128, 2, 32])
    s5f = sb("s5f", [128, 2, 32])
    d_s5f = Dep("s5f")
    h0s = sb("h0s", [128, 8, 32])
    fs = sb("fs", [128, 2, 8, 32])
    pcar = sb("pcar", [128, 4, 128])
    spT = sb("spT", [128, 4, 128])
    jt = sb("jt", [128, 128])
    pscale = sb("pscale", [128, NK])
    s5tmp = sb("s5tmp", [128, 2, 2, C])
    d_xx = [Dep("xx0"), Dep("xx1")]
    d_tab = [Dep("tab0"), Dep("tab1")]
    d_s5c = Dep("s5c")
    d_h0s = Dep("h0s")
    d_fs = Dep("fs")
    d_pcar = Dep("pcar")
    d_spT = Dep("spT")
    d_jt = Dep("jt")
    d_psc = Dep("pscale")
    MV, H0, F1, F2 = 0, 1, 2, 3

    s5w = nc.dram_tensor("s5w_scr", [32, 128, 640], F32).ap()
    s5t = nc.dram_tensor("s5t_scr", [32, 128, 2 * C], F32).ap()
    u_scr = nc.dram_tensor("u_scr", [2, 512, NT], F32).ap()
    y_scr = nc.dram_tensor("y_scr", [2, 512, NT], F32).ap()
    d_s5w = Dep("s5w")
    d_s5t = Dep("s5t")
    d_uscr = [Dep("uscr0"), Dep("uscr1")]
    d_yscr = [Dep("yscr0"), Dep("yscr1")]
    d_yq = [[Dep(f"yq{t_}_{c_}") for c_ in range(4)] for t_ in range(2)]
    d_yall = [Dep(f"yall{c_}") for c_ in range(4)]

    R1f = R1[:, :, :].bitcast(F32)
    R2f = R2[:, :, :].bitcast(F32)
    R2flat = R2f.rearrange("p k n -> p (k n)")
    WINS = (2, 4, 8, 16)
    EOFF = (0, 9, 20, 35)

    R2rflat = R2[:, :, :].rearrange("p k n -> p (k n)")

    def Ew(wg, typed=False):
        w = WINS[wg]
        rows = 8 + w - 1
        base = R2rflat if typed else R2flat
        return base[:, EOFF[wg] * C:(EOFF[wg] + rows) * C].rearrange("p (r c) -> p r c", c=C)

    fw.dma("sp", pscale[:], I["pool_scale"].rearrange("(k p) -> p k", p=128), writes=[d_psc], allow_slow_non_contiguous=True)
    act(jt[:, 0:64], ident[:, 64:128], AF.Copy, [d_const], [d_jt], scale=-1.0)
    act(jt[:, 64:128], ident[:, 0:64], AF.Copy, [d_const], [d_jt])
    memset("pool", pcar[:], 0.0, [d_pcar])
    memset("pool", s5c[:], 0.0, [d_s5c])
    for par_ in range(2):
        for gg_ in range(2):
            ts("dve", xx[:, par_, 1, gg_, :], pcar[:, 0:2, :].rearrange("p a n -> p (a n)")[:, 0:C], 0.0, None, ALU.mult, None, [d_pcar], [d_xx[par_]])
    memset("pool", spT[:], 0.0, [d_spT])

    def prologue():
        PI = math.pi
        arena = R2[:, :, :].rearrange("p k n -> p (k n)")
        o = [0]
        allocs = []
        order = []

        def alloc(n, at=None):
            if at is not None:
                o[0] = at
            assert o[0] + n <= NK * NT, (o, n)
            v = arena[:, o[0]:o[0] + n]
            dep = Dep(f"pro_{o[0]}")
            dep.w = gate_tok[0]
            allocs.append((o[0], o[0] + n, dep))
            order.append(dep)
            o[0] += n
            return v

        other = {"s5c": d_s5c, "ident": d_const, "small": d_small}

        def D(*aps):
            out = []
            for a in aps:
                if a is None or isinstance(a, (int, float)):
                    continue
                nm = a.tensor.name
                if nm == "R2":
                    off = a.offset % a.ap[0][0]
                    hit = [d for (lo, hi, d) in allocs if lo <= off < hi]
                    assert len(hit) >= 1, (nm, off)
                    out.append(hit[-1])
                elif nm in other:
                    out.append(other[nm])
            return out

        def R_(a):
            if isinstance(a, (int, float)) or a is None:
                return a
            return a.bitcast(F32) if a.dtype == F32R else a

        def T2(out, a, b, op):
            tt("dve", out, R_(a), R_(b), op, D(a, b), D(out))

        def TS(out, a, s1, op0, s2=None, op1=None):
            ts("dve", out, R_(a), s1, s2, op0, op1, D(a), D(out))

        def STT(out, in0, sc, in1, op0, op1, extra=()):
            stt("dve", out, R_(in0), R_(sc), R_(in1), op0, op1,
                D(in0, in1, sc if not isinstance(sc, (int, float)) else None) + list(extra), D(out))

        def CP_(out, in_, eng="dve", extra=()):
            copy(eng, out, R_(in_), D(in_) + list(extra), D(out))

        def DMAin(out, in_, **kw):
            return fw.dma("sp", out, in_, writes=D(out), **kw)

        gate_tok = [None]
        fw.op("dve", lambda e: e.tensor_scalar(out=small[:, 20:21], in0=small[:, 0:1], scalar1=0.0, scalar2=None, op0=ALU.mult),
              reads=[d_small], writes=list(d_R2) + [tdep("progate")])
        gate_tok[0] = tdep("progate").w

        def v32():
            return alloc(32)

        stgA = alloc(128)
        lamr, lami, lsb, dtb, xr, th, mag = v32(), v32(), v32(), v32(), v32(), v32(), v32()
        sn, cs_ = v32(), v32()
        ar, ai, den, nr, cr, ci, t0, t1 = v32(), v32(), v32(), v32(), v32(), v32(), v32(), v32()
        minv, er, ei = v32(), v32(), v32()
        pa, pb, pc = v32(), v32(), v32()
        q0, q1, q2 = v32(), v32(), v32()
        MAGIC = 12582912.0
        PR = alloc(9 * 32).rearrange("p (l g) -> p l g", g=32)
        PIm = alloc(9 * 32).rearrange("p (l g) -> p l g", g=32)
        PRr = alloc(8 * 32).rearrange("p (l g) -> p l g", g=32)
        PIr = alloc(8 * 32).rearrange("p (l g) -> p l g", g=32)
        X1b = alloc(512).rearrange("p (g h) -> p g h", h=16)
        X2b = alloc(512).rearrange("p (g h) -> p g h", h=16)
        BBr = alloc(512).rearrange("p (g h) -> p g h", h=16)
        BBi = alloc(512).rearrange("p (g h) -> p g h", h=16)
        tb_ = alloc(256)
        Cst = [alloc(128), alloc(128)]
        CrT = alloc(512).rearrange("p (g q) -> p g q", q=16)
        CiT = alloc(512).rearrange("p (g q) -> p g q", q=16)
        Ys = [alloc(32).rearrange("p (g q) -> p g q", q=16) for _ in range(4)]
        dvec = alloc(32)
        bmark = o[0]
        Bre = alloc(512).rearrange("p (g h) -> p g h", h=16)
        Bim = alloc(512).rearrange("p (g h) -> p g h", h=16)
        omark = o[0]

        def zero(v):
            ts("dve", v, ident[0:v.shape[0], 0:v.shape[1]], 0.0, None, ALU.mult, None, [d_const], D(v))

        def load_T(dst, src2d, st):
            zero(st)
            DMAin(st[0:32, 0:64], src2d)
            DMAin(st[0:32, 64:128], src2d)
            b, db = bank()
            tr(ps[:, b, 0:128], R_(st), ident[:], D(st) + [d_const], [db])
            copy("act", dst, ps[:, b, 0:32], [db], D(dst))

        load_T(lamr, I["lam_re"], stgA)
        load_T(lami, I["lam_im"], Cst[0])
        DMAin(lsb, I["log_step"].partition_broadcast(128))
        for half in range(2):
            fw.dma("act", Bre[half * 64:(half + 1) * 64], I["b_re"].rearrange("g p h -> p g h"), writes=D(Bre))
            fw.dma("act", Bim[half * 64:(half + 1) * 64], I["b_im"].rearrange("g p h -> p g h"), writes=D(Bim))
        for j in range(8):
            fw.dma("act", dvec[j * 16:(j + 1) * 16, :], I["d_skip"].rearrange("(g h) -> h g", h=16), writes=D(dvec),
                   allow_slow_non_contiguous=True)
        yield
        Cre_v = I["c_re"].rearrange("g q p -> (g q) p")
        Cim_v = I["c_im"].rearrange("g q p -> (g q) p")
        o[0] = omark
        Cst8 = [alloc(128) for _ in range(8)]
        o[0] = omark
        ci_ = 0
        for blk in range(4):
            for (src, dstT) in ((Cre_v, CrT), (Cim_v, CiT)):
                st = Cst8[ci_]
                ci_ += 1
                DMAin(st[:, 0:64], src[blk * 128:(blk + 1) * 128, :])
                DMAin(st[:, 64:128], src[blk * 128:(blk + 1) * 128, :])
        for ci_ in range(8):
            blk = ci_ // 2
            dstT = CrT if ci_ % 2 == 0 else CiT
            st = Cst8[ci_]
            b, db = bank()
            tr(ps[:, b, 0:128], R_(st), ident[:], D(st) + [d_const], [db])
            copy("act", dstT[:, blk * 8:(blk + 1) * 8, :], ps[:, b, 0:128].rearrange("p (g q) -> p g q", q=16), [db], D(dstT))
            yield
        deadC = D(*Cst8)

        def to_int(dst_f, src_f, kint=None):
            TS(dst_f, src_f, MAGIC, ALU.add)
            TS(dst_f, dst_f, -MAGIC, ALU.add)

        def pow2(dst, kf):
            TS(q0, kf, 32.0, ALU.add)
            for idx, i in enumerate((5, 4, 3, 2, 1, 0)):
                w_ = float(2 ** i)
                TS(q1, q0, w_, ALU.is_ge)
                STT(q0, q1, -w_, q0, ALU.mult, ALU.add)
                TS(q2, q1, -1.0, ALU.mult, 1.0, ALU.add)
                STT(q1, q1, float(2.0 ** (2 ** i)), q2, ALU.mult, ALU.add)
                if idx == 0:
                    CP_(dst, q1)
                else:
                    T2(dst, dst, q1, ALU.mult)
            TS(dst, dst, float(2.0 ** -32), ALU.mult)

        def exp_acc(dst, x):
            TS(pa, x, 1.0 / math.log(2.0), ALU.mult)
            to_int(pb, pa)
            STT(pc, pb, -0.693359375, x, ALU.mult, ALU.add)
            STT(pc, pb, 2.12194440e-4, pc, ALU.mult, ALU.add)
            yield
            TS(pa, pc, 1.0 / 12.0, ALU.mult, 1.0, ALU.add)
            for n in range(11, 0, -1):
                T2(pa, pa, pc, ALU.mult)
                TS(pa, pa, 1.0 / n, ALU.mult, 1.0, ALU.add)
                if n % 2 == 0:
                    yield
            yield
            pow2(pc, pb)
            yield
            T2(dst, pa, pc, ALU.mult)
            yield

        def sincos_acc(dst_s, dst_c, th_):
            C1, C2 = 6.28125, 2 * PI - 6.28125
            TS(pa, th_, 1.0 / (2 * PI), ALU.mult)
            to_int(pb, pa)
            STT(pc, pb, -C1, th_, ALU.mult, ALU.add)
            STT(pc, pb, -C2, pc, ALU.mult, ALU.add)
            yield
            for (thr, op, sg) in ((PI, ALU.is_gt, -1.0), (-PI, ALU.is_lt, 1.0)):
                TS(pa, pc, thr, op)
                STT(pc, pa, sg * C1, pc, ALU.mult, ALU.add)
                STT(pc, pa, sg * C2, pc, ALU.mult, ALU.add)
                yield
            TS(pc, pc, 0.25, ALU.mult)
            T2(pb, pc, pc, ALU.mult)
            sc = (1.0, -1 / 6.0, 1 / 120.0, -1 / 5040.0, 1 / 362880.0, -1 / 39916800.0, 1 / 6227020800.0)
            cc_ = (1.0, -0.5, 1 / 24.0, -1 / 720.0, 1 / 40320.0, -1 / 3628800.0, 1 / 479001600.0, -1 / 87178291200.0)
            TS(pa, pb, sc[-1], ALU.mult, sc[-2], ALU.add)
            for i_, cf in enumerate(sc[-3::-1]):
                T2(pa, pa, pb, ALU.mult)
                TS(pa, pa, cf, ALU.add)
                if i_ % 2 == 1:
                    yield
            T2(dst_s, pa, pc, ALU.mult)
            TS(pa, pb, cc_[-1], ALU.mult, cc_[-2], ALU.add)
            yield
            for i_, cf in enumerate(cc_[-3::-1]):
                T2(pa, pa, pb, ALU.mult)
                TS(pa, pa, cf, ALU.add)
                if i_ % 2 == 1:
                    yield
            CP_(dst_c, pa)
            for _ in range(2):
                T2(pa, dst_s, dst_c, ALU.mult)
                T2(pb, dst_s, dst_s, ALU.mult)
                T2(pc, dst_c, dst_c, ALU.mult)
                TS(dst_s, pa, 2.0, ALU.mult)
                T2(dst_c, pc, pb, ALU.subtract)
                yield

        yield from exp_acc(dtb, lsb)
        T2(xr, lamr, dtb, ALU.mult)
        T2(th, lami, dtb, ALU.mult)
        yield from exp_acc(mag, xr)
        T2(t0, mag, mag, ALU.mult)
        T2(t0, t0, t0, ALU.mult)
        T2(t0, t0, t0, ALU.mult)
        copy("dve", s5c[:, MV, :], R_(t0), D(t0) + [d_s5c], [d_s5c])
        rtmp = small[:, 32:64]
        recip(rtmp, R_(t0), D(t0) + [d_small], [d_small])
        copy("dve", minv, rtmp, [d_small], D(minv))
        yield from sincos_acc(sn, cs_, th)
        T2(ar, mag, cs_, ALU.mult)
        T2(ai, mag, sn, ALU.mult)
        T2(den, lamr, lamr, ALU.mult)
        T2(t0, lami, lami, ALU.mult)
        T2(den, den, t0, ALU.add)
        recip(rtmp, R_(den), D(den) + [d_small], [d_small])
        copy("dve", den, rtmp, [d_small], D(den))
        TS(nr, ar, -1.0, ALU.add)
        T2(t0, nr, lamr, ALU.mult)
        T2(t1, ai, lami, ALU.mult)
        T2(t0, t0, t1, ALU.add)
        T2(cr, t0, den, ALU.mult)
        T2(t0, ai, lamr, ALU.mult)
        T2(t1, nr, lami, ALU.mult)
        T2(t0, t0, t1, ALU.subtract)
        T2(ci, t0, den, ALU.mult)
        yield
        TS(PR[:, 0, :], ar, 0.0, ALU.mult, 1.0, ALU.add)
        TS(PIm[:, 0, :], ar, 0.0, ALU.mult)
        CP_(PR[:, 1, :], ar)
        CP_(PIm[:, 1, :], ai)
        for l in range(1, 8):
            T2(t0, PR[:, l, :], ar, ALU.mult)
            T2(t1, PIm[:, l, :], ai, ALU.mult)
            T2(PR[:, l + 1, :], t0, t1, ALU.subtract)
            T2(t0, PR[:, l, :], ai, ALU.mult)
            T2(t1, PIm[:, l, :], ar, ALU.mult)
            T2(PIm[:, l + 1, :], t0, t1, ALU.add)
            yield
        T2(er, PR[:, 8, :], minv, ALU.mult)
        T2(ei, PIm[:, 8, :], minv, ALU.mult)
        for l in range(8):
            CP_(PRr[:, l, :], PR[:, 7 - l, :])
            CP_(PIr[:, l, :], PIm[:, 7 - l, :])
        yield

        def bc(v):
            return v.unsqueeze(2).to_broadcast([128, 32, 16])

        T2(BBr, Bre, bc(cr), ALU.mult)
        T2(X1b, Bim, bc(ci), ALU.mult)
        T2(BBr, BBr, X1b, ALU.subtract)
        T2(BBi, Bim, bc(cr), ALU.mult)
        T2(X2b, Bre, bc(ci), ALU.mult)
        T2(BBi, BBi, X2b, ALU.add)
        yield
        CP_(X1b[0:64], BBr[0:64])
        TS(X2b[0:64], BBi[0:64], -1.0, ALU.mult)
        CP_(X1b[64:128], BBi[64:128])
        CP_(X2b[64:128], BBr[64:128])
        yield

        deadB = D(Bre, Bim)
        o[0] = bmark
        n0 = len(order)
        ZB = alloc(2 * 240).rearrange("p (g n) -> p g n", n=240)
        W1T = alloc(2 * 128).rearrange("p (g n) -> p g n", n=128)
        VL = alloc(2 * 128).rearrange("p (g n) -> p g n", n=128)
        assert o[0] <= omark
        o[0] = omark
        OUTW = alloc(2 * 640).rearrange("p (g n) -> p g n", n=640)
        fw.op("dve", lambda e: e.tensor_scalar(out=small[:, 20:21], in0=small[:, 0:1], scalar1=0.0, scalar2=None, op0=ALU.mult),
              reads=[d_small], writes=deadB + deadC + D(ZB, W1T, VL, OUTW) + [d_small])
        for g_ in range(2):
            zero(ZB[:, g_, 0:128])
            zero(ZB[:, g_, 112:240])
        Y1, Y2, Y1s, Y2s = Ys
        GB = 2

        def ap4(base, off, dims, typed=False):
            a = bass.AP(base.tensor, base.offset + off, [list(base.ap[0])] + [list(d) for d in dims])
            return a if typed else a.bitcast(F32)

        tball = ap4(tb_, 0, [[128, GB], [16, 8], [1, 16]], typed=True)

        def family(dst_base, dst_off, dst_gs, A1, A2, PWr, PWi, l0, g0):
            dst = ap4(dst_base, dst_off, [[dst_gs, GB], [16, 8], [1, 16]], typed=True)
            a1 = ap4(A1, 0, [[16, GB], [0, 8], [1, 16]])
            a2 = ap4(A2, 0, [[16, GB], [0, 8], [1, 16]])
            pwr = ap4(PWr, l0 * 32 + g0, [[1, GB], [32, 8], [0, 16]])
            pwi = ap4(PWi, l0 * 32 + g0, [[1, GB], [32, 8], [0, 16]])
            tt("dve", dst, a1, pwr, ALU.mult, D(A1, PWr), D(dst_base))
            tt("dve", tball, a2, pwi, ALU.mult, D(A2, PWi), D(tb_))
            tt("dve", dst, R_(dst), R_(tball), ALU.add, D(dst_base, tb_), D(dst_base))

        for g0 in range(0, 32, GB):
            gs_ = slice(g0, g0 + GB)
            CP_(ZB[0:64, :, 112:128], BBr[0:64, gs_, :])
            CP_(ZB[64:128, :, 112:128], BBi[64:128, gs_, :])
            CP_(Y1[0:64], CrT[0:64, gs_, :])
            TS(Y2[0:64], CiT[0:64, gs_, :], -1.0, ALU.mult)
            TS(Y1[64:128], CiT[64:128, gs_, :], -1.0, ALU.mult)
            TS(Y2[64:128], CrT[64:128, gs_, :], -1.0, ALU.mult)
            TS(Y1s[0:64], CiT[0:64, gs_, :], -1.0, ALU.mult)
            TS(Y2s[0:64], CrT[0:64, gs_, :], -1.0, ALU.mult)
            TS(Y1s[64:128], CrT[64:128, gs_, :], -1.0, ALU.mult)
            CP_(Y2s[64:128], CiT[64:128, gs_, :])
            yield
            family(W1T, 0, 128, X1b[:, gs_, :], X2b[:, gs_, :], PRr, PIr, 0, g0)
            family(VL, 0, 128, Y1, Y2, PR, PIm, 0, g0)
            yield
            family(OUTW, 384, 640, Y1, Y2, PR, PIm, 1, g0)
            family(OUTW, 512, 640, Y1s, Y2s, PR, PIm, 1, g0)
            yield
            for gi_ in range(GB):
                g = g0 + gi_
                b, db = bank()
                tr(ps[:, b, 0:128], R_(W1T[:, gi_, :]), ident[:], D(W1T) + [d_const], [db])
                copy("act", OUTW[:, gi_, 128:256], ps[:, b, 0:128], [db], D(OUTW))
                copy("act", OUTW[:, gi_, 256:320], ps[:, b, 64:128], [db], D(OUTW))
                act(OUTW[:, gi_, 320:384], ps[:, b, 0:64], AF.Copy, [db], D(OUTW), scale=-1.0)
                b2, db2 = bank()
                for j in range(8):
                    mm(ps[:, b2, 16 * j:128], R_(ZB[:, gi_, 112 - 16 * j:240 - 16 * j]), R_(VL[:, gi_, 0:128 - 16 * j]),
                       j == 0, j == 7, D(ZB, VL), [db2])
                stt("dve", OUTW[:, gi_, 0:128], ident[:], R_(dvec[:, g:g + 1]), ps[:, b2, 0:128], ALU.mult, ALU.add,
                    [db2, d_const] + D(dvec), D(OUTW))
                yield
            fw.dma("sp", s5w[g0:g0 + GB].rearrange("g p n -> p g n"), R_(OUTW), reads=D(OUTW), writes=[d_s5w])
            yield

        dead_all = [d for d in order if d not in D(er, ei)]
        o[0] = PR.offset % PR.ap[0][0]
        TG = 16
        tr_ = alloc(TG * 72).rearrange("p (g n) -> p g n", n=72)
        ti_ = alloc(TG * 72).rearrange("p (g n) -> p g n", n=72)
        TB = alloc(TG * 2 * C).rearrange("p (g a c) -> p g a c", a=2, c=C)
        fw.op("dve", lambda e: e.tensor_scalar(out=small[:, 20:21], in0=small[:, 0:1], scalar1=0.0, scalar2=None, op0=ALU.mult),
              reads=[d_small], writes=dead_all + D(tr_, ti_, TB) + [d_small])
        yield
        for g0 in range(0, 32, TG):
            gsl = slice(g0, g0 + TG)
            TR = TB[:, :, 0, :]
            TI = TB[:, :, 1, :]
            CP_(TR[:, :, 0:1], er[:, gsl].unsqueeze(2))
            CP_(TI[:, :, 0:1], ei[:, gsl].unsqueeze(2))
            n = 1
            while n < 128:
                pr_ = TR[:, :, n - 1:n].to_broadcast([128, TG, n])
                pi_ = TI[:, :, n - 1:n].to_broadcast([128, TG, n])
                T2(tr_[:, :, 0:n], TR[:, :, 0:n], pr_, ALU.mult)
                T2(ti_[:, :, 0:n], TI[:, :, 0:n], pi_, ALU.mult)
                yield
                T2(TR[:, :, n:2 * n], tr_[:, :, 0:n], ti_[:, :, 0:n], ALU.subtract)
                T2(tr_[:, :, 0:n], TR[:, :, 0:n], pi_, ALU.mult)
                yield
                T2(ti_[:, :, 0:n], TI[:, :, 0:n], pr_, ALU.mult)
                T2(TI[:, :, n:2 * n], tr_[:, :, 0:n], ti_[:, :, 0:n], ALU.add)
                n *= 2
                yield
            CP_(TR[:, :, 128:136], TR[:, :, 0:1].to_broadcast([128, TG, 8]))
            CP_(TI[:, :, 128:136], TI[:, :, 0:1].to_broadcast([128, TG, 8]))
            fw.dma("sp", s5t[g0:g0 + TG].rearrange("g p n -> p g n"), R_(TB.rearrange("p g a c -> p g (a c)")), reads=D(TB), writes=[d_s5t])
            yield
        allv = list(order)
        fw.op("dve", lambda e: e.tensor_scalar(out=small[:, 20:21], in0=small[:, 0:1], scalar1=0.0, scalar2=None, op0=ALU.mult),
              reads=[d_small], writes=list(d_R2) + allv + [d_small])
        yield

    PUMP = int(os.environ.get("KPUMP", "1"))
    PRO = {"gen": None}

    def pump(n):
        g_ = PRO["gen"]
        if g_ is None:
            return
        for _ in range(n):
            try:
                next(g_)
            except StopIteration:
                PRO["gen"] = None
                return

    def mixer(t):
        rmsnorm(1)
        uT = R1f[:, 0:4, :]
        pooled = R1[:, 4:8, :]
        yT = R1[:, 0:4, :]
        Uall = R1[:, 0:4, :].rearrange("p k n -> p (k n)").rearrange("p (g c) -> p g c", c=C)
        Yall = R2f[:, 4:8, :].rearrange("p k n -> p (k n)").rearrange("p (g c) -> p g c", c=C)
        YallT = R2[:, 4:8, :].rearrange("p k n -> p (k n)").rearrange("p (g c) -> p g c", c=C)
        win_v = I["w_in"].rearrange("(k p) n -> p k n", p=128)

        for oc in range(8):
            wv, dw = wload(win_v[:, :, oc * 128:(oc + 1) * 128], [NK, 128])
            for s in range(NSUB):
                sl = slice(s * SUB, (s + 1) * SUB)
                b, db = bank()
                for k in range(NK):
                    mm(ps[:, b, 0:SUB], wv[:, k, :], hn[:, k, sl], k == 0, k == NK - 1, [dw, d_hn], [db])
                if oc < 4:
                    copy(alt(), R1[:, oc, sl], ps[:, b, 0:SUB], [db], [d_R1[oc]])
                else:
                    wg = oc - 4
                    w = WINS[wg]
                    copy(alt(), Ew(wg, True)[:, w - 1 + 2 * s:w + 1 + 2 * s, :], ps[:, b, 0:SUB].rearrange("p (r c) -> p r c", c=C),
                         [db], d_R2)
        for ch in range(4):
            fw.dma("sp", u_scr[t, ch * 128:(ch + 1) * 128, :], uT[:, ch, :], reads=[d_R1[ch]], writes=[d_uscr[t]])

        for j in range(8):
            src = u_scr[t].rearrange("(g h) (j c) -> j h g c", h=16, c=C)[j]
            fw.dma("pool", Uall[j * 16:(j + 1) * 16, :, :], src.bitcast(F32R), reads=[d_uscr[t]], writes=d_R1[0:4])

        for hb in range(2):
            sth = TMP[:, (hb % 2) * 128:(hb % 2) * 128 + 128]
            dsth = tdep("stg0")
            fw.dma("sp", sth[:, 0:64], I["st_re"][8 * t + 4 * hb:8 * t + 4 * hb + 4].rearrange("s g p -> (s g) p"), writes=[dsth])
            fw.dma("sp", sth[:, 64:128], I["st_im"][8 * t + 4 * hb:8 * t + 4 * hb + 4].rearrange("s g p -> (s g) p"), writes=[dsth])
            b, db = bank()
            tr(ps[:, b, 0:128], sth, ident[:], [dsth, d_const], [db])
            copy("dve", h0s[:, 4 * hb:4 * hb + 4, :], ps[:, b, 0:128].rearrange("p (s g) -> p s g", g=32), [db], [d_h0s])

        stp = TMP[:, 2048:3072]
        dstp = [tdep(f"stgs{j}") for j in range(8)]
        dstq = [tdep(f"stp{i}") for i in range(15)]
        ts("dve", TMP[:, 2048:2560], pcar[:, :, :].rearrange("p a n -> p (a n)"), 0.0, None, ALU.mult, None, [d_pcar], dstp + dstq)
        for i in range(15):
            fw.dma("sp", stp[i * 8:(i + 1) * 8, 0:512], I["st_pool"][8 * t:8 * t + 8, i, :], writes=[dstq[i]])
        b, db = bank()
        for wg in range(4):
            tr(ps[:, b, wg * 128:(wg + 1) * 128], stp[:, wg * 128:(wg + 1) * 128], ident[:], dstp + dstq + [d_const], [db])
        copy("dve", spT[:, :, :], ps[:, b, :].rearrange("p (a n) -> p a n", n=128), [db], [d_spT])
        for cs in range(8):
            out_toks.append(fw.dma("sp", O["o_spool"][8 * t + cs, 0:7, :], I["st_pool"][8 * t + cs, 8:15, :], reads=[tdep("d2d")]))
        for wg in range(4):
            w = WINS[wg]
            Ev = Ew(wg)
            Et = Ew(wg, True)
            copy(alt(), Et[:, 0:w - 1, 0], pcar[:, wg, 16 - w:15], d_R2 + [d_pcar], d_R2)
            for r in range(w - 1):
                copy(alt(), Et[:, r, CP:C], spT[:, wg, (16 - w + r) * 8:(17 - w + r) * 8], d_R2 + [d_spT], d_R2)
            if w == 16:
                copy(alt(), Et[:, 7:15, 1:CP], Ev[:, 15:23, 0:CP - 1], d_R2, d_R2)
                copy(alt(), Et[:, 0:7, 1:CP], Ev[:, 8:15, 0:CP - 1], d_R2, d_R2)
            else:
                copy(alt(), Et[:, 0:w - 1, 1:CP], Ev[:, 8:w + 7, 0:CP - 1], d_R2, d_R2)
        upT = TMP[:, 3072:3584].rearrange("p (a n) -> p a n", n=128)
        dup = tdep("xsc")
        for wg in range(4):
            w = WINS[wg]
            Ev = Ew(wg)
            copy(alt(), pcar[:, wg, 0:7], Ev[:, w:w + 7, CP - 2], d_R2, [d_pcar])
            copy(alt(), pcar[:, wg, 7:15], Ev[:, w - 1:w + 7, CP - 1], d_R2, [d_pcar])
            for j in range(8):
                copy(alt(), upT[:, wg, j * 8:(j + 1) * 8], Ev[:, w - 1 + j, CP:C], d_R2 + [d_zpad], [dup])
        b, db = bank()
        for wg in range(4):
            tr(ps[:, b, wg * 128:(wg + 1) * 128], upT[:, wg, :], ident[:], [dup, d_const], [db])
        ots = TMP[:, 0:512]
        dots = tdep("stg0")
        copy("dve", ots, ps[:, b, :], [db], [dots])
        for j in range(8):
            out_toks.append(fw.dma("sp", O["o_spool"][8 * t:8 * t + 8, 7 + j, :], ots[j * 8:(j + 1) * 8, :], reads=[dots]))
        if t == 1:
            b, db = bank()
            for wg in range(4):
                tr(ps[:, b, wg * 128:(wg + 1) * 128], pcar[:, wg, :], ident[:], [d_pcar, d_const], [db])
            otp = TMP[:, 512:1024]
            dotp = tdep("stg1")
            copy("dve", otp, ps[:, b, :], [db], [dotp])
            out_toks.append(fw.dma("sp", O["o_ppool"][:, :], otp[0:15, :], reads=[dotp]))

        Tt = TMP[:, 1024:1024 + 22 * C].rearrange("p (r c) -> p r c", c=C)
        dT = [tdep("stg1"), tdep("stgs0"), tdep("stgs1"), tdep("stgs2"), tdep("stgs3"), tdep("stgs4"), tdep("stgs5"),
              tdep("stgs6"), tdep("stgs7"), tdep("xsc")]
        for wg in range(4):
            w = WINS[wg]
            Ev = Ew(wg)
            R = 8 + w - 1
            tt("dve", Tt[:, 0:R - 1, :], Ev[:, 0:R - 1, :], Ev[:, 1:R, :], ALU.add, d_R2, dT)
            n = R - 1
            sh = 2
            while sh < w:
                tt("dve", Tt[:, 0:n - sh, :], Tt[:, 0:n - sh, :], Tt[:, sh:n, :], ALU.add, dT, dT)
                n -= sh
                sh *= 2
            stt("dve", pooled[:, wg, :].rearrange("p (r c) -> p r c", c=C), Tt[:, 0:8, :], 1.0 / w, Ev[:, w - 1:w + 7, :],
                ALU.mult, ALU.subtract, dT + d_R2, [d_R1[4 + wg]])
            if t == 0:
                for tau in range(w - 1):
                    j, c0 = tau % 8, tau // 8
                    stt("dve", pooled[:, wg, j * C + c0:j * C + c0 + 1], Tt[:, j, c0:c0 + 1], 1.0 / (tau + 1),
                        Ev[:, w - 1 + j, c0:c0 + 1], ALU.mult, ALU.subtract, dT + d_R2, [d_R1[4 + wg]])

        S5A = TMP[:, 1024:1024 + 6 * C].rearrange("p (a g c) -> p a g c", a=3, c=C)
        dS5A = dT
        gfend = TMP[:, 2048:2048 + 288].rearrange("p (g c) -> p g c", c=9)
        tend = TMP[:, 2336:2336 + 576].rearrange("p (g a c) -> p g a c", a=2, c=9)
        d_gfend = tdep("stgs0")
        stq_all = [tdep(f"stgs{j}") for j in range(8)] + [tdep(f"stp{i}") for i in range(15)]
        for a_ in range(2):
            fw.dma("sp", tend[:, :, a_, :], s5t.rearrange("g p (a c) -> p g a c", a=2)[:, :, a_, CP - 1:C], reads=[d_s5t],
                   writes=[tdep(f"tend{a_}")] + (stq_all if a_ == 0 else []))
        d_tend = tdep("tend0")
        d_tend1 = tdep("tend1")
        pair = {}

        def s5_front(pi_):
            g = 2 * pi_
            par = pi_ % 2
            wsl0, dws0 = wload(s5w[g], [5, 128], reads=[d_s5w])
            wsl1, dws1 = wload(s5w[g + 1], [5, 128], reads=[d_s5w])
            tabv = tabr[:, par, :, :]
            fw.dma("sp", tabv, s5t[g:g + 2].rearrange("g p n -> p g n"), reads=[d_s5t], writes=[d_tab[par]])
            bS = 4 * par
            dbS = [d_bank[bS], d_bank[bS + 1]]
            dbS2 = [d_bank[bS + 2], d_bank[bS + 3]]
            wsl = (wsl0, wsl1)
            dws = (dws0, dws1)
            for gg in range(2):
                U = Uall[:, g + gg, :]
                mm(ps[:, bS + gg, 0:C], wsl[gg][:, 1, :], U, True, True, [dws[gg], d_R1[g // 8]], [dbS[gg]])
                mm(ps[:, bS + 2 + gg, 0:C], wsl[gg][:, 2, :], U, True, True, [dws[gg], d_R1[g // 8]], [dbS2[gg]])
            pair[pi_] = (wsl, dws, tabv, bS, dbS, dbS2)

        s5_front(0)
        pend = []
        for pi_ in range(16):
            g = 2 * pi_
            par = pi_ % 2
            wsl, dws, tabv, bS, dbS, dbS2 = pair.pop(pi_)
            cosT = tabv[:, :, 0:C]
            sinT = tabv[:, :, C:2 * C]
            dU = d_R1[0:4]
            t1 = S5A[:, 0, :, :]
            t2 = S5A[:, 1, :, :]
            sp_ = S5A[:, 2, :, :]
            gf = s5tmp[:, par, :, :]
            dG = tdep(f"s5g{par}")
            X1 = xx[:, par, 0, :, :]
            X2 = xx[:, par, 1, :, :]
            tt("dve", t1, ps[:, bS:bS + 2, 0:C], cosT, ALU.mult, dbS + [d_tab[par]], dS5A)
            tt("dve", t2, ps[:, bS + 2:bS + 4, 0:C], sinT, ALU.mult, dbS2 + [d_tab[par]], dS5A)
            if pi_ + 1 < 16:
                s5_front(pi_ + 1)
            tt("dve", sp_, t1, t2, ALU.add, dS5A, dS5A)
            for gg in range(2):
                mcol = s5c[:, MV, g + gg:g + gg + 1]
                fw.op("dve", (lambda o_, d0, d1, ini: lambda e: e.tensor_tensor_scan(
                    out=o_, data0=d0, data1=d1, initial=ini, op0=ALU.mult, op1=ALU.add))(
                    gf[:, gg, 0:CP], mcol.to_broadcast([128, CP]), sp_[:, gg, 0:CP], s5c[:, H0, g + gg:g + gg + 1]),
                    reads=dS5A + [d_s5c], writes=[dG])
                stt("dve", gf[:, gg, CP:C], h0s[:, :, g + gg], mcol, sp_[:, gg, CP:C], ALU.mult, ALU.add,
                    dS5A + [d_h0s, d_s5c], [dG])
            tt("dve", X1[:, :, 1:CP], cosT[:, :, 0:CP - 1], gf[:, :, 0:CP - 1], ALU.mult, [dG, d_tab[par]], [d_xx[par]])
            tt("dve", X2[:, :, 1:CP], sinT[:, :, 0:CP - 1], gf[:, :, 0:CP - 1], ALU.mult, [dG, d_tab[par]], [d_xx[par]])
            copy("act", X1[:, :, 0:1], s5c[:, H0, g:g + 2].unsqueeze(2), [d_s5c], [d_xx[par]])
            for gg in range(2):
                copy("act", X1[:, gg, CP:C], h0s[:, :, g + gg], [d_h0s], [d_xx[par]])
            for gg in range(2):
                U = Uall[:, g + gg, :]
                mm(ps[:, bS + gg, 0:C], wsl[gg][:, 0, :], U, True, False, [dws[gg], d_R1[g // 8]], [dbS[gg]])
                mm(ps[:, bS + gg, 0:C], wsl[gg][:, 3, :], X1[:, gg, :], False, False, [dws[gg], d_xx[par]], [dbS[gg]])
                mm(ps[:, bS + gg, 0:C], wsl[gg][:, 4, :], X2[:, gg, :], False, True, [dws[gg], d_xx[par]], [dbS[gg]])
            copy("dve", gfend[:, g:g + 2, :], gf[:, :, CP - 1:C], [dG], [d_gfend] + (stq_all if pi_ == 0 else []))
            act(YallT[:, g:g + 2, :], ps[:, bS:bS + 2, 0:C], AF.Gelu_apprx_tanh, dbS,
                [d_yall[pi_ // 4]] + (d_R2[4:8] if pi_ == 0 else []))
            if pi_ % 4 == 3:
                ch = pi_ // 4
                for k in range(8):
                    dst = y_scr[t].rearrange("(g q) (k c) -> k q g c", q=16, c=C)[k][:, 8 * ch:8 * ch + 8, :]
                    fw.dma("act", dst, Yall[k * 16:(k + 1) * 16, 8 * ch:8 * ch + 8, :], reads=[d_yall[ch]], writes=[d_yq[t][ch]])
                pend.append((pi_ + 2, ch))
            while pend and (pend[0][0] <= pi_ or pi_ == 15):
                _, ch_ = pend.pop(0)
                fw.dma("pool", yT[:, ch_, :], y_scr[t, ch_ * 128:(ch_ + 1) * 128, :].bitcast(F32R), reads=[d_yq[t][ch_]], writes=[d_R1[ch_]])
        tt("dve", s5f[:, 0, :], tend[:, :, 0, 0], gfend[:, :, 0], ALU.mult, [d_tend, d_tend1, d_gfend], [d_s5f])
        tt("dve", s5f[:, 1, :], tend[:, :, 1, 0], gfend[:, :, 0], ALU.mult, [d_tend, d_tend1, d_gfend], [d_s5f])
        tt("dve", fs[:, 0, :, :].rearrange("p s g -> p g s"), tend[:, :, 0, 1:9], gfend[:, :, 1:9], ALU.mult, [d_tend, d_tend1, d_gfend], [d_fs])
        tt("dve", fs[:, 1, :, :].rearrange("p s g -> p g s"), tend[:, :, 1, 1:9], gfend[:, :, 1:9], ALU.mult, [d_tend, d_tend1, d_gfend], [d_fs])
        ts("dve", small[:, 21:22], small[:, 0:1], 0.0, None, ALU.mult, None, [d_small], d_yall + d_R2[4:8] + [d_small])
        b, db = bank()
        mm(ps[:, b, 0:32], ident[:], s5f[:, 0, :], True, False, [d_s5f, d_const], [db])
        mm(ps[:, b, 0:32], jt[:], s5f[:, 1, :], False, True, [d_s5f, d_jt], [db])
        copy("dve", s5c[:, H0, :], ps[:, b, 0:32], [db], [d_s5c])
        b2, db2 = bank()
        mm(ps[:, b2, 0:256], ident[:], fs[:, 0, :, :].rearrange("p s g -> p (s g)"), True, False, [d_fs, d_const], [db2])
        mm(ps[:, b2, 0:256], jt[:], fs[:, 1, :, :].rearrange("p s g -> p (s g)"), False, True, [d_fs, d_jt], [db2])
        hsf = TMP[:, 0:256]
        dhs = tdep("stg0")
        copy("dve", hsf, ps[:, b2, 0:256], [db2], [dhs])
        b3, db3 = bank()
        tr(ps[:, b3, 0:128], hsf[:, 0:128], ident[:], [dhs, d_const], [db3])
        tr(ps[:, b3, 128:256], hsf[:, 128:256], ident[:], [dhs, d_const], [db3])
        hso = TMP[:, 256:512]
        copy("dve", hso, ps[:, b3, 0:256], [db3], [dhs])
        for cs in range(8):
            blk, r0 = cs // 4, (cs % 4) * 32
            out_toks.append(fw.dma("sp", O["o_sre"][8 * t + cs], hso[r0:r0 + 32, blk * 128:blk * 128 + 64], reads=[dhs]))
            out_toks.append(fw.dma("sp", O["o_sim"][8 * t + cs], hso[r0:r0 + 32, blk * 128 + 64:blk * 128 + 128], reads=[dhs]))
        if t == 1:
            hp = TMP[:, 512:640]
            dhp = tdep("stg1")
            memset("pool", hp, 0.0, [dhp])
            copy("dve", hp[:, 0:32], s5c[:, H0, :], [d_s5c], [dhp])
            b4, db4 = bank()
            tr(ps[:, b4, 0:128], hp, ident[:], [dhp, d_const], [db4])
            hpo = TMP[:, 640:768]
            copy("dve", hpo, ps[:, b4, 0:128], [db4], [dhp])
            out_toks.append(fw.dma("sp", O["o_pre"][:, :], hpo[0:32, 0:64], reads=[dhp]))
            out_toks.append(fw.dma("sp", O["o_pim"][:, :], hpo[0:32, 64:128], reads=[dhp]))

        glu_v = I["w_glu"].rearrange("(k p) n -> p k n", p=128)
        MT = TMP[:, 0:3 * SUB].rearrange("p (a n) -> p a n", n=SUB)
        dMT = [tdep("stg0"), tdep("stg1")]
        for m in range(NK):
            wgs, dgs = wload(win_v[:, :, 1024 + m * 128:1024 + (m + 1) * 128], [NK, 128])
            wgp, dgp = wload(win_v[:, :, 2048 + m * 128:2048 + (m + 1) * 128], [NK, 128])
            wa, da = wload(glu_v[:, :, m * 128:(m + 1) * 128], [4, 128])
            wb_, dwb = wload(glu_v[:, :, 1024 + m * 128:1024 + (m + 1) * 128], [4, 128])
            wgm = m // 2
            wp, dwp = wload(I["w_pool"][wgm, :, (m % 2) * 128:(m % 2 + 1) * 128], [128])
            for s in range(NSUB):
                sl = slice(s * SUB, (s + 1) * SUB)
                bA, dA_ = bank()
                bB, dB_ = bank()
                bS, dSg = bank()
                bP, dPg = bank()
                bY, dY_ = bank()
                for k in range(NK):
                    mm(ps[:, bS, 0:SUB], wgs[:, k, :], hn[:, k, sl], k == 0, k == NK - 1, [dgs, d_hn], [dSg])
                for k in range(NK):
                    mm(ps[:, bP, 0:SUB], wgp[:, k, :], hn[:, k, sl], k == 0, k == NK - 1, [dgp, d_hn], [dPg])
                mm(ps[:, bY, 0:SUB], wp, pooled[:, wgm, sl], True, True, [dwp, d_R1[4 + wgm]], [dY_])
                for k in range(4):
                    mm(ps[:, bB, 0:SUB], wb_[:, k, :], yT[:, k, sl], k == 0, k == 3, [dwb, d_R1[k]], [dB_])
                for k in range(4):
                    mm(ps[:, bA, 0:SUB], wa[:, k, :], yT[:, k, sl], k == 0, k == 3, [da, d_R1[k]], [dA_])
                act(MT[:, 0, :], ps[:, bB, 0:SUB], AF.Sigmoid, [dB_], dMT)
                act(MT[:, 1, :], ps[:, bS, 0:SUB], AF.Sigmoid, [dSg], dMT)
                act(MT[:, 2, :], ps[:, bP, 0:SUB], AF.Sigmoid, [dPg], dMT)
                tt("dve", MT[:, 0, :], ps[:, bA, 0:SUB], MT[:, 0, :], ALU.mult, [dA_] + dMT, dMT)
                tt("dve", MT[:, 0, :], MT[:, 0, :], MT[:, 1, :], ALU.mult, dMT, dMT)
                stt("dve", MT[:, 2, :], ps[:, bY, 0:SUB], pscale[:, m:m + 1], MT[:, 2, :], ALU.mult, ALU.mult, [dY_, d_psc] + dMT, dMT)
                tt("dve", R2[:, m, sl], MT[:, 0, :], MT[:, 2, :], ALU.add, dMT, [d_R2[m]])

        for k in range(NK):
            copy(alt(), hn[:, k, :], R2[:, k, :], [d_R2[k]], [d_hn])
        wo_v = I["w_out"].rearrange("(k p) n -> p k n", p=128)
        for m in range(NK):
            wo, dwo = wload(wo_v[:, :, m * 128:(m + 1) * 128], [NK, 128])
            for s in range(NSUB):
                sl = slice(s * SUB, (s + 1) * SUB)
                b, db = bank()
                for k in range(NK):
                    mm(ps[:, b, 0:SUB], wo[:, k, :], hn[:, k, sl], k == 0, k == NK - 1, [dwo, d_hn], [db])
                tt("dve", xT[:, m, sl], ps[:, b, 0:SUB], xT[:, m, sl], ALU.add, [db, d_xT[m]], [d_xT[m]])

    kst = os.environ.get("KSTAGE", "full")
    kparts = os.environ.get("KPARTS", "load,store").split(",")
    if kst in ("full", "mix", "pro"):
        PRO["gen"] = prologue()
        pump(9)
        if kst in ("pro", "mix"):
            pump(100000)
    zero_pads()
    for t in range(int(os.environ.get("KT", "2"))):
        if "load" in kparts:
            load_tile(t)
        if kst in ("full", "ffn1", "nomix"):
            ffn(0, I["ffn1_up"], I["ffn1_down"])
        if kst in ("full", "mix"):
            pump(100000)
            mixer(t)
        if kst in ("full", "nomix"):
            ffn(2, I["ffn2_up"], I["ffn2_down"])
        if "store" in kparts:
            final_store(t, t == 0)
    if kst == "pro":
        out_toks.extend([d_s5w.w, d_s5t.w])
    elif "store" not in kparts:
        out_toks.append(fw.dma("sp", O["yp"][0:128, :], xT[:, 0, 0:1024], reads=d_xT))

    fw.finish_wait("sp", out_toks)
    fw.emit()
    es.close()
    return nc


_CACHE = {}


def kernel(**inputs):
    f = lambda a: np.ascontiguousarray(np.asarray(a, dtype=np.float32))
    shared = {
        "ffn1_norm": f(inputs["ffn1_norm"][0]), "ffn1_up": f(inputs["ffn1_up"][0]), "ffn1_down": f(inputs["ffn1_down"][0]),
        "mix_norm": f(inputs["mix_norm"][0]), "w_in": f(inputs["w_in"][0]),
        "lam_re": f(inputs["lam_re"][0]), "lam_im": f(inputs["lam_im"][0]), "log_step": f(inputs["log_step"][0]),
        "b_re": f(inputs["b_re"][0]), "b_im": f(inputs["b_im"][0]), "c_re": f(inputs["c_re"][0]), "c_im": f(inputs["c_im"][0]),
        "d_skip": f(inputs["d_skip"][0]), "w_glu": f(inputs["w_glu"][0]), "w_pool": f(inputs["w_pool"][0]),
        "pool_scale": f(inputs["pool_scale"][0]), "w_out": f(inputs["w_out"][0]),
        "ffn2_norm": f(inputs["ffn2_norm"][0]), "ffn2_up": f(inputs["ffn2_up"][0]), "ffn2_down": f(inputs["ffn2_down"][0]),
        "final_norm": f(inputs["final_norm"]),
    }
    xp = f(inputs["x_prompt"])
    xs = f(inputs["x_sample"])
    sre = f(inputs["state_ssm_re"][0])
    sim = f(inputs["state_ssm_im"][0])
    spool = f(inputs["state_pool"][0])
    in_maps = []
    for c in range(8):
        m = dict(shared)
        m["xp"] = xp[c]
        m["xs"] = xs[16 * c:16 * c + 16]
        m["st_re"] = sre[16 * c:16 * c + 16]
        m["st_im"] = sim[16 * c:16 * c + 16]
        m["st_pool"] = spool[16 * c:16 * c + 16]
        in_maps.append(m)
    if "nc" not in _CACHE:
        _CACHE["nc"] = build_program()
    res = run_bass_kernel_spmd(_CACHE["nc"], in_maps, core_ids=list(range(8)))
    r = res.results
    y_prompt = np.stack([r[c]["yp"] for c in range(8)])
    y_sample = np.concatenate([r[c]["ys"] for c in range(8)], axis=0)
    p_re = np.stack([r[c]["o_pre"] for c in range(8)])[None]
    p_im = np.stack([r[c]["o_pim"] for c in range(8)])[None]
    p_pool = np.stack([r[c]["o_ppool"] for c in range(8)])[None]
    s_re = np.concatenate([r[c]["o_sre"] for c in range(8)], axis=0)[None]
    s_im = np.concatenate([r[c]["o_sim"] for c in range(8)], axis=0)[None]
    s_pool = np.concatenate([r[c]["o_spool"] for c in range(8)], axis=0)[None]
    return (y_prompt, y_sample, p_re, p_im, p_pool, s_re, s_im, s_pool)
```

```python
import math
import os
from contextlib import ExitStack

import numpy as np
import concourse.bass as bass
import concourse.mybir as mybir
from concourse.bass_utils import run_bass_kernel_spmd

F32 = mybir.dt.float32
F32R = mybir.dt.float32r
I32 = mybir.dt.int32
AF = mybir.ActivationFunctionType
ALU = mybir.AluOpType

ENGS = ("pe", "act", "dve", "pool", "sp")


class Dep:
    __slots__ = ("name", "w", "r", "dsem", "dcnt", "swc")

    def __init__(self, name=""):
        self.name = name
        self.w = None
        self.r = []
        self.dsem = None
        self.dcnt = 0
        self.swc = None


class FW:
    def __init__(self, nc, same_engine_sync=False):
        self.nc = nc
        self.ops = {e: [] for e in ENGS}
        self.seq = {e: 0 for e in ENGS}
        self.same = same_engine_sync
        self.esem = {}
        self.waited = {e: {} for e in ENGS}

    def _collect(self, eng, toks):
        best = {}
        for tok in toks:
            if tok is None:
                continue
            if tok[0] == "eng":
                _, e2, s, hz = tok
                if e2 == eng and (eng == "pe" or not (self.same or hz)):
                    continue
                key = ("eng", e2)
                val = s
            else:
                key = ("dma", id(tok[1]))
                val = tok[2]
            if self.waited[eng].get(key, -1) >= val:
                continue
            if key not in best or best[key][0] < val:
                best[key] = (val, tok)
        waits = []
        for key, (val, tok) in best.items():
            self.waited[eng][key] = val
            waits.append(tok)
        return waits

    def op(self, eng, fn, reads=(), writes=(), dma=False, hz=False):
        cands = []
        for d in reads:
            cands.append(d.w)
        for d in writes:
            cands.append(d.w)
            cands.extend(d.r)
        waits = self._collect(eng, cands)
        seq = self.seq[eng]
        self.seq[eng] += 1
        rec = {"waits": waits, "fn": fn, "dma": None, "seq": seq}
        if dma:
            tgt = writes[0] if writes else reads[0]
            if eng == "pool":
                if tgt.swc is None:
                    tgt.swc = Dep(tgt.name + "_sw")
                tgt = tgt.swc
            tgt.dcnt += 16
            tok = ("dma", tgt, tgt.dcnt)
            rec["dma"] = tgt
        else:
            tok = ("eng", eng, seq, hz)
        for d in reads:
            d.r.append(tok)
        for d in writes:
            d.w = tok
            d.r = []
        self.ops[eng].append(rec)
        return tok

    def dma(self, eng, out, in_, reads=(), writes=(), **kw):
        return self.op(eng, lambda e: e.dma_start(out=out, in_=in_, **kw), reads, writes, dma=True)

    def finish_wait(self, eng, toks):
        waits = self._collect(eng, toks)
        self.ops[eng].append({"waits": waits, "fn": None, "dma": None, "seq": None})

    def emit(self):
        nc = self.nc
        needed = {e: set() for e in ENGS}
        ddeps = {}
        for e in ENGS:
            for rec in self.ops[e]:
                for t in rec["waits"]:
                    if t[0] == "eng":
                        needed[t[1]].add(t[2])
                    else:
                        ddeps[id(t[1])] = t[1]
                if rec["dma"] is not None:
                    ddeps[id(rec["dma"])] = rec["dma"]
        val_at = {}
        for e in ENGS:
            c = 0
            m = {}
            for s in range(self.seq[e]):
                if s in needed[e]:
                    c += 1
                m[s] = c
            val_at[e] = m
        for e in ENGS:
            self.esem[e] = nc.alloc_semaphore(name=f"sem_{e}")
        for i, d in enumerate(ddeps.values()):
            d.dsem = nc.alloc_semaphore(name=f"dsem_{i}")

        def run(e, h):
            for rec in self.ops[e]:
                for t in rec["waits"]:
                    if t[0] == "eng":
                        h.wait_ge(self.esem[t[1]], val_at[t[1]][t[2]])
                    else:
                        h.wait_ge(t[1].dsem, t[2])
                if rec["fn"] is None:
                    continue
                ins = rec["fn"](h)
                if rec["dma"] is not None:
                    ins.then_inc(rec["dma"].dsem, 16)
                elif rec["seq"] in needed[e]:
                    ins.then_inc(self.esem[e], 1)

        with nc.Block() as block:
            @block.tensor
            def _(h):
                run("pe", h)

            @block.scalar
            def _(h):
                run("act", h)

            @block.vector
            def _(h):
                run("dve", h)

            @block.gpsimd
            def _(h):
                run("pool", h)

            @block.sync
            def _(h):
                run("sp", h)


D = 1024
DFF = 2816
NK = 8
NF = 22
NT = 1088
C = 136
CP = 128
CS = 8
SUB = 272
NSUB = 4
EPS = 1e-6
NSLOT = 7

IN_SPECS = [
    ("xp", [2048, 1024]), ("xs", [16, 8, 1024]),
    ("st_re", [16, 32, 64]), ("st_im", [16, 32, 64]), ("st_pool", [16, 15, 512]),
    ("ffn1_norm", [1024]), ("ffn1_up", [1024, 5632]), ("ffn1_down", [2816, 1024]),
    ("mix_norm", [1024]), ("w_in", [1024, 3072]),
    ("lam_re", [32, 64]), ("lam_im", [32, 64]), ("log_step", [32]),
    ("b_re", [32, 64, 16]), ("b_im", [32, 64, 16]), ("c_re", [32, 16, 64]), ("c_im", [32, 16, 64]),
    ("d_skip", [512]), ("w_glu", [512, 2048]), ("w_pool", [4, 128, 256]), ("pool_scale", [1024]),
    ("w_out", [1024, 1024]), ("ffn2_norm", [1024]), ("ffn2_up", [1024, 5632]), ("ffn2_down", [2816, 1024]),
    ("final_norm", [1024]),
]
OUT_SPECS = [
    ("yp", [2048, 1024]), ("ys", [16, 8, 1024]),
    ("o_pre", [32, 64]), ("o_pim", [32, 64]), ("o_ppool", [15, 512]),
    ("o_sre", [16, 32, 64]), ("o_sim", [16, 32, 64]), ("o_spool", [16, 15, 512]),
]

STAGE = {"mixer": False}


def build_program():
    nc = bass.Bass("TRN2", target_bir_lowering=False)
    I = {n: nc.dram_tensor(n, s, F32, kind="ExternalInput").ap() for n, s in IN_SPECS}
    O = {n: nc.dram_tensor(n, s, F32, kind="ExternalOutput").ap() for n, s in OUT_SPECS}
    fw = FW(nc, same_engine_sync=True)
    es = ExitStack()

    def sb(name, shape, dt=F32):
        return es.enter_context(nc.sbuf_tensor(name, shape, dt))

    xT = sb("xT", [128, NK, NT])
    hn = sb("hn", [128, NK, NT], F32R)
    R1 = sb("R1", [128, NK, NT], F32R)
    ring = sb("ring", [128, NSLOT, 1024], F32R)
    TMP = sb("TMP", [128, 5120])
    ident = sb("ident", [128, 128])
    ones = sb("ones", [128, 128], F32R)
    onesf = sb("onesf", [128, 128])
    gains = sb("gains", [128, 4, NK])
    small = sb("small", [128, 128])
    sqt = sb("sqt", [128, 2, SUB], F32R)
    ps = es.enter_context(nc.psum_tensor("ps", [128, 8, 512], F32))

    d_xT = [Dep(f"xT{k}") for k in range(NK)]
    d_hn = Dep("hn")
    d_R1 = [Dep(f"R1_{k}") for k in range(NK)]
    d_ring = [Dep(f"ring{i}") for i in range(NSLOT)]
    d_bank = [Dep(f"bank{i}") for i in range(8)]
    d_const = Dep("const")
    d_gain = Dep("gain")
    d_small = Dep("small")
    d_tmp = {}

    def tdep(name):
        if name not in d_tmp:
            d_tmp[name] = Dep(name)
        return d_tmp[name]

    state = {"bank": 0, "slot": 0, "flip": 0}

    def bank():
        b = state["bank"]
        state["bank"] = (b + 1) % 8
        return b, d_bank[b]

    def wload(src_ap, view_shape, reads=()):
        s = state["slot"]
        state["slot"] = (s + 1) % NSLOT
        n = 1
        for v in view_shape:
            n *= v
        dst = ring[:, s, 0:n]
        if len(view_shape) == 2:
            dst = dst.rearrange("p (a b) -> p a b", b=view_shape[1])
        fw.dma("pool", dst, src_ap.bitcast(F32R), reads=list(reads), writes=[d_ring[s]])
        return dst, d_ring[s]

    def mm(out, lhsT, rhs, start, stop, reads, writes):
        fw.op("pe", lambda e: e.matmul(out, lhsT, rhs, start=start, stop=stop), reads=reads, writes=writes)

    def tr(out, in_, idn, reads, writes):
        fw.op("pe", lambda e: e.transpose(out=out, in_=in_, identity=idn), reads=reads, writes=writes)

    def act(out, in_, func, reads, writes, **kw):
        fw.op("act", lambda e: e.activation(out=out, in_=in_, func=func, **kw), reads=reads, writes=writes)

    def tt(eng, out, in0, in1, op, reads, writes):
        fw.op(eng, lambda e: e.tensor_tensor(out=out, in0=in0, in1=in1, op=op), reads=reads, writes=writes)

    def stt(eng, out, in0, scalar, in1, op0, op1, reads, writes):
        fw.op(eng, lambda e: e.scalar_tensor_tensor(out=out, in0=in0, scalar=scalar, in1=in1, op0=op0, op1=op1),
              reads=reads, writes=writes)

    def ts(eng, out, in0, s1, s2, op0, op1, reads, writes):
        if op1 is None:
            fw.op(eng, lambda e: e.tensor_scalar(out=out, in0=in0, scalar1=s1, scalar2=None, op0=op0), reads=reads, writes=writes)
        else:
            fw.op(eng, lambda e: e.tensor_scalar(out=out, in0=in0, scalar1=s1, scalar2=s2, op0=op0, op1=op1),
                  reads=reads, writes=writes)

    def recip(out, in_, reads, writes):
        fw.op("dve", lambda e: e.reciprocal(out=out, in_=in_), reads=reads, writes=writes)

    def memset(eng, ap, val, writes):
        fw.op(eng, lambda e: e.memset(ap, val), writes=writes, hz=True)

    def alt():
        state["flip"] ^= 1
        return "act" if state["flip"] else "dve"

    def copy(eng, out, in_, reads, writes):
        if eng == "act":
            act(out, in_, AF.Copy, reads, writes)
        else:
            fw.op(eng, lambda e: e.tensor_copy(out=out, in_=in_), reads=reads, writes=writes)

    memset("pool", ident[:], 0.0, [d_const])
    fw.op("pool", lambda e: e.affine_select(out=ident[:], in_=ident[:], pattern=[[-1, 128]],
                                            compare_op=ALU.not_equal, fill=1.0, base=0, channel_multiplier=1),
          reads=[d_const], writes=[d_const], hz=True)
    memset("pool", onesf[:], 1.0, [d_const])
    act(ones[:], onesf[:], AF.Copy, [d_const], [d_const])
    for i, nm in enumerate(["ffn1_norm", "mix_norm", "ffn2_norm"]):
        fw.dma("sp", gains[:, i, :], I[nm].rearrange("(k p) -> p k", p=128), writes=[d_gain],
               allow_slow_non_contiguous=True)
    eps_t = small[:, 0:1]
    memset("pool", small[:, 0:1], EPS, [d_small])
    d_zpad = Dep("zpad")

    def zero_pads():
        memset("pool", TMP[64:128, 2048:3072], 0.0, [d_zpad])
        memset("pool", TMP[:, 3072:4096], 0.0, [d_zpad])

    xp_v = I["xp"].rearrange("(t c j) d -> t j c d", t=2, c=CP, j=8)
    yp_v = O["yp"].rearrange("(t c j) d -> t j c d", t=2, c=CP, j=8)

    def xsamp(k):
        return xT[:, k, :].rearrange("p (j c) -> p j c", c=C)[:, :, CP:C]

    def load_tile(t):
        ksub = os.environ.get("KSUB", "ps")
        for j in range(8 if "p" in ksub else 0):
            stg = TMP[:, (j % 2) * 1024:(j % 2 + 1) * 1024]
            dstg = tdep(f"stg{j % 2}")
            fw.dma("sp", stg, xp_v[t, j], writes=[dstg])
            for h in range(2):
                b, db = bank()
                for kk in range(4):
                    k = h * 4 + kk
                    tr(ps[:, b, kk * 128:(kk + 1) * 128], stg[:, k * 128:(k + 1) * 128], ident[:], [dstg, d_const], [db])
                copy(alt(), xT[:, h * 4:(h + 1) * 4, j * C:j * C + CP],
                     ps[:, b, :].rearrange("p (k c) -> p k c", c=128), [db], d_xT[h * 4:(h + 1) * 4])
        if "s" not in ksub:
            return
        stg = TMP[:, 2048:3072]
        sdeps = [tdep(f"stgs{j}") for j in range(8)]
        for j in range(8):
            fw.dma("sp", stg[j * 8:(j + 1) * 8, :], I["xs"][8 * t:8 * t + 8, j, :], reads=[d_zpad], writes=[sdeps[j]])
        kcuts = int(os.environ.get("KCUTS", "2"))
        for h in range(2 if kcuts >= 1 else 0):
            b, db = bank()
            for kk in range(4):
                k = h * 4 + kk
                tr(ps[:, b, kk * 128:(kk + 1) * 128], stg[:, k * 128:(k + 1) * 128], ident[:], sdeps + [d_const, d_zpad], [db])
            for kk in range(4 if kcuts >= 2 else 0):
                k = h * 4 + kk
                for j in range(8):
                    copy(alt(), xT[:, k, j * C + CP:(j + 1) * C], ps[:, b, kk * 128 + j * 8:kk * 128 + j * 8 + 8], [db], [d_xT[k]])

    def rmsnorm(gi):
        for s in range(NSUB):
            sl = slice(s * SUB, (s + 1) * SUB)
            b, db = bank()
            for k in range(NK):
                sq = sqt[:, k % 2, :]
                dsq = tdep(f"sq{k % 2}")
                act(sq, xT[:, k, sl], AF.Square, [d_xT[k]], [dsq])
                mm(ps[:, b, 0:SUB], ones[:], sq, k == 0, k == NK - 1, [dsq, d_const], [db])
            sd = TMP[:, 1024:1024 + SUB]
            dsd = tdep("stg1")
            act(sd, ps[:, b, 0:SUB], AF.Sqrt, [db, d_small], [dsd], scale=1.0 / D, bias=eps_t)
            recip(sd, sd, [dsd], [dsd])
            for k in range(NK):
                stt("dve", hn[:, k, sl], xT[:, k, sl], gains[:, gi, k:k + 1], sd, ALU.mult, ALU.mult,
                    [d_xT[k], dsd, d_gain], [d_hn])

    def ffn(gi, w_up, w_down):
        rmsnorm(gi)
        up_v = w_up.rearrange("(k p) n -> p k n", p=128)
        for (f0, f1) in [(0, 8), (8, 15), (15, 22)]:
            nf = f1 - f0
            for f in range(f0, f1):
                wg, dg = wload(up_v[:, :, f * 128:(f + 1) * 128], [NK, 128])
                wu, du = wload(up_v[:, :, DFF + f * 128:DFF + (f + 1) * 128], [NK, 128])
                for s in range(NSUB):
                    sl = slice(s * SUB, (s + 1) * SUB)
                    ba, dba = bank()
                    bb, dbb = bank()
                    for k in range(NK):
                        mm(ps[:, ba, 0:SUB], wg[:, k, :], hn[:, k, sl], k == 0, k == NK - 1, [dg, d_hn], [dba])
                    for k in range(NK):
                        mm(ps[:, bb, 0:SUB], wu[:, k, :], hn[:, k, sl], k == 0, k == NK - 1, [du, d_hn], [dbb])
                    sg = TMP[:, (s % 2) * SUB:(s % 2 + 1) * SUB]
                    dsg = tdep(f"sg{s % 2}")
                    act(sg, ps[:, ba, 0:SUB], AF.Silu, [dba], [dsg])
                    tt("dve", R1[:, f - f0, sl], sg, ps[:, bb, 0:SUB], ALU.mult, [dsg, dbb], [d_R1[f - f0]])
                    pump(PUMP)
            dn_v = w_down[f0 * 128:f1 * 128, :].rearrange("(f p) n -> p f n", p=128)
            for m in range(NK):
                wd, dd = wload(dn_v[:, :, m * 128:(m + 1) * 128], [nf, 128])
                for s in range(NSUB):
                    sl = slice(s * SUB, (s + 1) * SUB)
                    b, db = bank()
                    for i in range(nf):
                        mm(ps[:, b, 0:SUB], wd[:, i, :], R1[:, i, sl], i == 0, i == nf - 1, [dd, d_R1[i]], [db])
                    stt("dve", xT[:, m, sl], ps[:, b, 0:SUB], 0.5, xT[:, m, sl], ALU.mult, ALU.add, [db, d_xT[m]], [d_xT[m]])
                    pump(PUMP)

    gfin = TMP[:, 4096:5120]
    d_gfin = Dep("gfin")
    out_toks = []
    d_out = Dep("out")

    def norm_rows(np_, banks, dbs, ot, dot, slot):
        col = 64 + 4 * slot
        dcol = tdep(f"fin{slot}")
        ssq = small[0:np_, col:col + 2]
        for h in range(2):
            act(ot[:, h * 512:(h + 1) * 512], ps[0:np_, banks[h], :], AF.Square, [dbs[h]], dot + [dcol],
                accum_out=ssq[:, h:h + 1])
        rs = small[0:np_, col + 2:col + 3]
        tt("dve", rs, ssq[:, 0:1], ssq[:, 1:2], ALU.add, [dcol], [dcol])
        act(rs, rs, AF.Sqrt, [dcol, d_small], [dcol], scale=1.0 / D, bias=eps_t[0:np_, :])
        recip(rs, rs, [dcol], [dcol])
        for h in range(2):
            stt("dve", ot[:, h * 512:(h + 1) * 512], ps[0:np_, banks[h], :], rs, gfin[0:np_, h * 512:(h + 1) * 512],
                ALU.mult, ALU.mult, [dbs[h], dcol, d_gfin], dot)

    def final_store(t, first):
        if first:
            fw.dma("sp", gfin, I["final_norm"].partition_broadcast(128), writes=[d_gfin])
        for j in range(8):
            banks, dbs = [], []
            for h in range(2):
                b, db = bank()
                banks.append(b)
                dbs.append(db)
                for kk in range(4):
                    k = h * 4 + kk
                    tr(ps[:, b, kk * 128:(kk + 1) * 128], xT[:, k, j * C:j * C + CP], ident[:], [d_xT[k], d_const], [db])
            ot = TMP[:, (j % 2) * 1024:(j % 2 + 1) * 1024]
            dot = tdep(f"stg{j % 2}")
            norm_rows(128, banks, dbs, ot, [dot], j)
            out_toks.append(fw.dma("sp", yp_v[t, j], ot, reads=[dot]))
        banks, dbs = [], []
        xsc = TMP[:, 3072:4096].rearrange("p (k n) -> p k n", n=128)
        dxsc = tdep("xsc")
        for k in range(NK):
            for j in range(8):
                copy("act", xsc[:, k, j * 8:(j + 1) * 8], xT[:, k, j * C + CP:(j + 1) * C], [d_xT[k], d_zpad], [dxsc])
        for h in range(2):
            b, db = bank()
            banks.append(b)
            dbs.append(db)
            for kk in range(4):
                k = h * 4 + kk
                tr(ps[:, b, kk * 128:(kk + 1) * 128], xsc[:, k, :], ident[:], [dxsc, d_const], [db])
        ot = TMP[0:64, 2048:3072]
        dots = [tdep(f"stgs{j}") for j in range(8)]
        norm_rows(64, banks, dbs, ot, dots, 8)
        for j in range(8):
            out_toks.append(fw.dma("sp", O["ys"][8 * t:8 * t + 8, j, :], ot[j * 8:(j + 1) * 8, :], reads=[dots[j]]))

    R2 = sb("R2", [128, NK, NT], F32)
    d_R2 = [Dep(f"R2_{k}") for k in range(NK)]
    xx = sb("xx", [128, 2, 2, 2, C], F32R)
    tabr = sb("tabr", [128, 2, 2, 2 * C])
    s5c = sb("s5c", [128, 2, 32])
    s5f = sb("s5f", [128, 2, 32])
    d_s5f = Dep("s5f")
    h0s = sb("h0s", [128, 8, 32])
    fs = sb("fs", [128, 2, 8, 32])
    pcar = sb("pcar", [128, 4, 128])
    spT = sb("spT", [128, 4, 128])
    jt = sb("jt", [128, 128])
    pscale = sb("pscale", [128, NK])
    s5tmp = sb("s5tmp", [128, 2, 2, C])
    d_xx = [Dep("xx0"), Dep("xx1")]
    d_tab = [Dep("tab0"), Dep("tab1")]
    d_s5c = Dep("s5c")
    d_h0s = Dep("h0s")
    d_fs = Dep("fs")
    d_pcar = Dep("pcar")
    d_spT = Dep("spT")
    d_jt = Dep("jt")
    d_psc = Dep("pscale")
    MV, H0, F1, F2 = 0, 1, 2, 3

    s5w = nc.dram_tensor("s5w_scr", [32, 128, 640], F32).ap()
    s5t = nc.dram_tensor("s5t_scr", [32, 128, 2 * C], F32).ap()
    u_scr = nc.dram_tensor("u_scr", [2, 512, NT], F32).ap()
    y_scr = nc.dram_tensor("y_scr", [2, 512, NT], F32).ap()
    d_s5w = Dep("s5w")
    d_s5t = Dep("s5t")
    d_uscr = [Dep("uscr0"), Dep("uscr1")]
    d_yscr = [Dep("yscr0"), Dep("yscr1")]
    d_yq = [[Dep(f"yq{t_}_{c_}") for c_ in range(4)] for t_ in range(2)]
    d_yall = [Dep(f"yall{c_}") for c_ in range(4)]

    R1f = R1[:, :, :].bitcast(F32)
    R2f = R2[:, :, :].bitcast(F32)
    R2flat = R2f.rearrange("p k n -> p (k n)")
    WINS = (2, 4, 8, 16)
    EOFF = (0, 9, 20, 35)

    R2rflat = R2[:, :, :].rearrange("p k n -> p (k n)")

    def Ew(wg, typed=False):
        w = WINS[wg]
        rows = 8 + w - 1
        base = R2rflat if typed else R2flat
        return base[:, EOFF[wg] * C:(EOFF[wg] + rows) * C].rearrange("p (r c) -> p r c", c=C)

    fw.dma("sp", pscale[:], I["pool_scale"].rearrange("(k p) -> p k", p=128), writes=[d_psc], allow_slow_non_contiguous=True)
    act(jt[:, 0:64], ident[:, 64:128], AF.Copy, [d_const], [d_jt], scale=-1.0)
    act(jt[:, 64:128], ident[:, 0:64], AF.Copy, [d_const], [d_jt])
    memset("pool", pcar[:], 0.0, [d_pcar])
    memset("pool", s5c[:], 0.0, [d_s5c])
    for par_ in range(2):
        for gg_ in range(2):
            ts("dve", xx[:, par_, 1, gg_, :], pcar[:, 0:2, :].rearrange("p a n -> p (a n)")[:, 0:C], 0.0, None, ALU.mult, None, [d_pcar], [d_xx[par_]])
    memset("pool", spT[:], 0.0, [d_spT])

    def prologue():
        PI = math.pi
        arena = R2[:, :, :].rearrange("p k n -> p (k n)")
        o = [0]
        allocs = []
        order = []

        def alloc(n, at=None):
            if at is not None:
                o[0] = at
            assert o[0] + n <= NK * NT, (o, n)
            v = arena[:, o[0]:o[0] + n]
            dep = Dep(f"pro_{o[0]}")
            dep.w = gate_tok[0]
            allocs.append((o[0], o[0] + n, dep))
            order.append(dep)
            o[0] += n
            return v

        other = {"s5c": d_s5c, "ident": d_const, "small": d_small}

        def D(*aps):
            out = []
            for a in aps:
                if a is None or isinstance(a, (int, float)):
                    continue
                nm = a.tensor.name
                if nm == "R2":
                    off = a.offset % a.ap[0][0]
                    hit = [d for (lo, hi, d) in allocs if lo <= off < hi]
                    assert len(hit) >= 1, (nm, off)
                    out.append(hit[-1])
                elif nm in other:
                    out.append(other[nm])
            return out

        def R_(a):
            if isinstance(a, (int, float)) or a is None:
                return a
            return a.bitcast(F32) if a.dtype == F32R else a

        def T2(out, a, b, op):
            tt("dve", out, R_(a), R_(b), op, D(a, b), D(out))

        def TS(out, a, s1, op0, s2=None, op1=None):
            ts("dve", out, R_(a), s1, s2, op0, op1, D(a), D(out))

        def STT(out, in0, sc, in1, op0, op1, extra=()):
            stt("dve", out, R_(in0), R_(sc), R_(in1), op0, op1,
                D(in0, in1, sc if not isinstance(sc, (int, float)) else None) + list(extra), D(out))

        def CP_(out, in_, eng="dve", extra=()):
            copy(eng, out, R_(in_), D(in_) + list(extra), D(out))

        def DMAin(out, in_, **kw):
            return fw.dma("sp", out, in_, writes=D(out), **kw)

        gate_tok = [None]
        fw.op("dve", lambda e: e.tensor_scalar(out=small[:, 20:21], in0=small[:, 0:1], scalar1=0.0, scalar2=None, op0=ALU.mult),
              reads=[d_small], writes=list(d_R2) + [tdep("progate")])
        gate_tok[0] = tdep("progate").w

        def v32():
            return alloc(32)

        stgA = alloc(128)
        lamr, lami, lsb, dtb, xr, th, mag = v32(), v32(), v32(), v32(), v32(), v32(), v32()
        sn, cs_ = v32(), v32()
        ar, ai, den, nr, cr, ci, t0, t1 = v32(), v32(), v32(), v32(), v32(), v32(), v32(), v32()
        minv, er, ei = v32(), v32(), v32()
        pa, pb, pc = v32(), v32(), v32()
        q0, q1, q2 = v32(), v32(), v32()
        MAGIC = 12582912.0
        PR = alloc(9 * 32).rearrange("p (l g) -> p l g", g=32)
        PIm = alloc(9 * 32).rearrange("p (l g) -> p l g", g=32)
        PRr = alloc(8 * 32).rearrange("p (l g) -> p l g", g=32)
        PIr = alloc(8 * 32).rearrange("p (l g) -> p l g", g=32)
        X1b = alloc(512).rearrange("p (g h) -> p g h", h=16)
        X2b = alloc(512).rearrange("p (g h) -> p g h", h=16)
        BBr = alloc(512).rearrange("p (g h) -> p g h", h=16)
        BBi = alloc(512).rearrange("p (g h) -> p g h", h=16)
        tb_ = alloc(256)
        Cst = [alloc(128), alloc(128)]
        CrT = alloc(512).rearrange("p (g q) -> p g q", q=16)
        CiT = alloc(512).rearrange("p (g q) -> p g q", q=16)
        Ys = [alloc(32).rearrange("p (g q) -> p g q", q=16) for _ in range(4)]
        dvec = alloc(32)
        bmark = o[0]
        Bre = alloc(512).rearrange("p (g h) -> p g h", h=16)
        Bim = alloc(512).rearrange("p (g h) -> p g h", h=16)
        omark = o[0]

        def zero(v):
            ts("dve", v, ident[0:v.shape[0], 0:v.shape[1]], 0.0, None, ALU.mult, None, [d_const], D(v))

        def load_T(dst, src2d, st):
            zero(st)
            DMAin(st[0:32, 0:64], src2d)
            DMAin(st[0:32, 64:128], src2d)
            b, db = bank()
            tr(ps[:, b, 0:128], R_(st), ident[:], D(st) + [d_const], [db])
            copy("act", dst, ps[:, b, 0:32], [db], D(dst))

        load_T(lamr, I["lam_re"], stgA)
        load_T(lami, I["lam_im"], Cst[0])
        DMAin(lsb, I["log_step"].partition_broadcast(128))
        for half in range(2):
            fw.dma("act", Bre[half * 64:(half + 1) * 64], I["b_re"].rearrange("g p h -> p g h"), writes=D(Bre))
            fw.dma("act", Bim[half * 64:(half + 1) * 64], I["b_im"].rearrange("g p h -> p g h"), writes=D(Bim))
        for j in range(8):
            fw.dma("act", dvec[j * 16:(j + 1) * 16, :], I["d_skip"].rearrange("(g h) -> h g", h=16), writes=D(dvec),
                   allow_slow_non_contiguous=True)
        yield
        Cre_v = I["c_re"].rearrange("g q p -> (g q) p")
        Cim_v = I["c_im"].rearrange("g q p -> (g q) p")
        o[0] = omark
        Cst8 = [alloc(128) for _ in range(8)]
        o[0] = omark
        ci_ = 0
        for blk in range(4):
            for (src, dstT) in ((Cre_v, CrT), (Cim_v, CiT)):
                st = Cst8[ci_]
                ci_ += 1
                DMAin(st[:, 0:64], src[blk * 128:(blk + 1) * 128, :])
                DMAin(st[:, 64:128], src[blk * 128:(blk + 1) * 128, :])
        for ci_ in range(8):
            blk = ci_ // 2
            dstT = CrT if ci_ % 2 == 0 else CiT
            st = Cst8[ci_]
            b, db = bank()
            tr(ps[:, b, 0:128], R_(st), ident[:], D(st) + [d_const], [db])
            copy("act", dstT[:, blk * 8:(blk + 1) * 8, :], ps[:, b, 0:128].rearrange("p (g q) -> p g q", q=16), [db], D(dstT))
            yield
        deadC = D(*Cst8)

        def to_int(dst_f, src_f, kint=None):
            TS(dst_f, src_f, MAGIC, ALU.add)
            TS(dst_f, dst_f, -MAGIC, ALU.add)

        def pow2(dst, kf):
            TS(q0, kf, 32.0, ALU.add)
            for idx, i in enumerate((5, 4, 3, 2, 1, 0)):
                w_ = float(2 ** i)
                TS(q1, q0, w_, ALU.is_ge)
                STT(q0, q1, -w_, q0, ALU.mult, ALU.add)
                TS(q2, q1, -1.0, ALU.mult, 1.0, ALU.add)
                STT(q1, q1, float(2.0 ** (2 ** i)), q2, ALU.mult, ALU.add)
                if idx == 0:
                    CP_(dst, q1)
                else:
                    T2(dst, dst, q1, ALU.mult)
            TS(dst, dst, float(2.0 ** -32), ALU.mult)

        def exp_acc(dst, x):
            TS(pa, x, 1.0 / math.log(2.0), ALU.mult)
            to_int(pb, pa)
            STT(pc, pb, -0.693359375, x, ALU.mult, ALU.add)
            STT(pc, pb, 2.12194440e-4, pc, ALU.mult, ALU.add)
            yield
            TS(pa, pc, 1.0 / 12.0, ALU.mult, 1.0, ALU.add)
            for n in range(11, 0, -1):
                T2(pa, pa, pc, ALU.mult)
                TS(pa, pa, 1.0 / n, ALU.mult, 1.0, ALU.add)
                if n % 2 == 0:
                    yield
            yield
            pow2(pc, pb)
            yield
            T2(dst, pa, pc, ALU.mult)
            yield

        def sincos_acc(dst_s, dst_c, th_):
            C1, C2 = 6.28125, 2 * PI - 6.28125
            TS(pa, th_, 1.0 / (2 * PI), ALU.mult)
            to_int(pb, pa)
            STT(pc, pb, -C1, th_, ALU.mult, ALU.add)
            STT(pc, pb, -C2, pc, ALU.mult, ALU.add)
            yield
            for (thr, op, sg) in ((PI, ALU.is_gt, -1.0), (-PI, ALU.is_lt, 1.0)):
                TS(pa, pc, thr, op)
                STT(pc, pa, sg * C1, pc, ALU.mult, ALU.add)
                STT(pc, pa, sg * C2, pc, ALU.mult, ALU.add)
                yield
            TS(pc, pc, 0.25, ALU.mult)
            T2(pb, pc, pc, ALU.mult)
            sc = (1.0, -1 / 6.0, 1 / 120.0, -1 / 5040.0, 1 / 362880.0, -1 / 39916800.0, 1 / 6227020800.0)
            cc_ = (1.0, -0.5, 1 / 24.0, -1 / 720.0, 1 / 40320.0, -1 / 3628800.0, 1 / 479001600.0, -1 / 87178291200.0)
            TS(pa, pb, sc[-1], ALU.mult, sc[-2], ALU.add)
            for i_, cf in enumerate(sc[-3::-1]):
                T2(pa, pa, pb, ALU.mult)
                TS(pa, pa, cf, ALU.add)
                if i_ % 2 == 1:
                    yield
            T2(dst_s, pa, pc, ALU.mult)
            TS(pa, pb, cc_[-1], ALU.mult, cc_[-2], ALU.add)
            yield
            for i_, cf in enumerate(cc_[-3::-1]):
                T2(pa, pa, pb, ALU.mult)
                TS(pa, pa, cf, ALU.add)
                if i_ % 2 == 1:
                    yield
            CP_(dst_c, pa)
            for _ in range(2):
                T2(pa, dst_s, dst_c, ALU.mult)
                T2(pb, dst_s, dst_s, ALU.mult)
                T2(pc, dst_c, dst_c, ALU.mult)
                TS(dst_s, pa, 2.0, ALU.mult)
                T2(dst_c, pc, pb, ALU.subtract)
                yield

        yield from exp_acc(dtb, lsb)
        T2(xr, lamr, dtb, ALU.mult)
        T2(th, lami, dtb, ALU.mult)
        yield from exp_acc(mag, xr)
        T2(t0, mag, mag, ALU.mult)
        T2(t0, t0, t0, ALU.mult)
        T2(t0, t0, t0, ALU.mult)
        copy("dve", s5c[:, MV, :], R_(t0), D(t0) + [d_s5c], [d_s5c])
        rtmp = small[:, 32:64]
        recip(rtmp, R_(t0), D(t0) + [d_small], [d_small])
        copy("dve", minv, rtmp, [d_small], D(minv))
        yield from sincos_acc(sn, cs_, th)
        T2(ar, mag, cs_, ALU.mult)
        T2(ai, mag, sn, ALU.mult)
        T2(den, lamr, lamr, ALU.mult)
        T2(t0, lami, lami, ALU.mult)
        T2(den, den, t0, ALU.add)
        recip(rtmp, R_(den), D(den) + [d_small], [d_small])
        copy("dve", den, rtmp, [d_small], D(den))
        TS(nr, ar, -1.0, ALU.add)
        T2(t0, nr, lamr, ALU.mult)
        T2(t1, ai, lami, ALU.mult)
        T2(t0, t0, t1, ALU.add)
        T2(cr, t0, den, ALU.mult)
        T2(t0, ai, lamr, ALU.mult)
        T2(t1, nr, lami, ALU.mult)
        T2(t0, t0, t1, ALU.subtract)
        T2(ci, t0, den, ALU.mult)
        yield
        TS(PR[:, 0, :], ar, 0.0, ALU.mult, 1.0, ALU.add)
        TS(PIm[:, 0, :], ar, 0.0, ALU.mult)
        CP_(PR[:, 1, :], ar)
        CP_(PIm[:, 1, :], ai)
        for l in range(1, 8):
            T2(t0, PR[:, l, :], ar, ALU.mult)
            T2(t1, PIm[:, l, :], ai, ALU.mult)
            T2(PR[:, l + 1, :], t0, t1, ALU.subtract)
            T2(t0, PR[:, l, :], ai, ALU.mult)
            T2(t1, PIm[:, l, :], ar, ALU.mult)
            T2(PIm[:, l + 1, :], t0, t1, ALU.add)
            yield
        T2(er, PR[:, 8, :], minv, ALU.mult)
        T2(ei, PIm[:, 8, :], minv, ALU.mult)
        for l in range(8):
            CP_(PRr[:, l, :], PR[:, 7 - l, :])
            CP_(PIr[:, l, :], PIm[:, 7 - l, :])
        yield

        def bc(v):
            return v.unsqueeze(2).to_broadcast([128, 32, 16])

        T2(BBr, Bre, bc(cr), ALU.mult)
        T2(X1b, Bim, bc(ci), ALU.mult)
        T2(BBr, BBr, X1b, ALU.subtract)
        T2(BBi, Bim, bc(cr), ALU.mult)
        T2(X2b, Bre, bc(ci), ALU.mult)
        T2(BBi, BBi, X2b, ALU.add)
        yield
        CP_(X1b[0:64], BBr[0:64])
        TS(X2b[0:64], BBi[0:64], -1.0, ALU.mult)
        CP_(X1b[64:128], BBi[64:128])
        CP_(X2b[64:128], BBr[64:128])
        yield

        deadB = D(Bre, Bim)
        o[0] = bmark
        n0 = len(order)
        ZB = alloc(2 * 240).rearrange("p (g n) -> p g n", n=240)
        W1T = alloc(2 * 128).rearrange("p (g n) -> p g n", n=128)
        VL = alloc(2 * 128).rearrange("p (g n) -> p g n", n=128)
        assert o[0] <= omark
        o[0] = omark
        OUTW = alloc(2 * 640).rearrange("p (g n) -> p g n", n=640)
        fw.op("dve", lambda e: e.tensor_scalar(out=small[:, 20:21], in0=small[:, 0:1], scalar1=0.0, scalar2=None, op0=ALU.mult),
              reads=[d_small], writes=deadB + deadC + D(ZB, W1T, VL, OUTW) + [d_small])
        for g_ in range(2):
            zero(ZB[:, g_, 0:128])
            zero(ZB[:, g_, 112:240])
        Y1, Y2, Y1s, Y2s = Ys
        GB = 2

        def ap4(base, off, dims, typed=False):
            a = bass.AP(base.tensor, base.offset + off, [list(base.ap[0])] + [list(d) for d in dims])
            return a if typed else a.bitcast(F32)

        tball = ap4(tb_, 0, [[128, GB], [16, 8], [1, 16]], typed=True)

        def family(dst_base, dst_off, dst_gs, A1, A2, PWr, PWi, l0, g0):
            dst = ap4(dst_base, dst_off, [[dst_gs, GB], [16, 8], [1, 16]], typed=True)
            a1 = ap4(A1, 0, [[16, GB], [0, 8], [1, 16]])
            a2 = ap4(A2, 0, [[16, GB], [0, 8], [1, 16]])
            pwr = ap4(PWr, l0 * 32 + g0, [[1, GB], [32, 8], [0, 16]])
            pwi = ap4(PWi, l0 * 32 + g0, [[1, GB], [32, 8], [0, 16]])
            tt("dve", dst, a1, pwr, ALU.mult, D(A1, PWr), D(dst_base))
            tt("dve", tball, a2, pwi, ALU.mult, D(A2, PWi), D(tb_))
            tt("dve", dst, R_(dst), R_(tball), ALU.add, D(dst_base, tb_), D(dst_base))

        for g0 in range(0, 32, GB):
            gs_ = slice(g0, g0 + GB)
            CP_(ZB[0:64, :, 112:128], BBr[0:64, gs_, :])
            CP_(ZB[64:128, :, 112:128], BBi[64:128, gs_, :])
            CP_(Y1[0:64], CrT[0:64, gs_, :])
            TS(Y2[0:64], CiT[0:64, gs_, :], -1.0, ALU.mult)
            TS(Y1[64:128], CiT[64:128, gs_, :], -1.0, ALU.mult)
            TS(Y2[64:128], CrT[64:128, gs_, :], -1.0, ALU.mult)
            TS(Y1s[0:64], CiT[0:64, gs_, :], -1.0, ALU.mult)
            TS(Y2s[0:64], CrT[0:64, gs_, :], -1.0, ALU.mult)
            TS(Y1s[64:128], CrT[64:128, gs_, :], -1.0, ALU.mult)
            CP_(Y2s[64:128], CiT[64:128, gs_, :])
            yield
            family(W1T, 0, 128, X1b[:, gs_, :], X2b[:, gs_, :], PRr, PIr, 0, g0)
            family(VL, 0, 128, Y1, Y2, PR, PIm, 0, g0)
            yield
            family(OUTW, 384, 640, Y1, Y2, PR, PIm, 1, g0)
            family(OUTW, 512, 640, Y1s, Y2s, PR, PIm, 1, g0)
            yield
            for gi_ in range(GB):
                g = g0 + gi_
                b, db = bank()
                tr(ps[:, b, 0:128], R_(W1T[:, gi_, :]), ident[:], D(W1T) + [d_const], [db])
                copy("act", OUTW[:, gi_, 128:256], ps[:, b, 0:128], [db], D(OUTW))
                copy("act", OUTW[:, gi_, 256:320], ps[:, b, 64:128], [db], D(OUTW))
                act(OUTW[:, gi_, 320:384], ps[:, b, 0:64], AF.Copy, [db], D(OUTW), scale=-1.0)
                b2, db2 = bank()
                for j in range(8):
                    mm(ps[:, b2, 16 * j:128], R_(ZB[:, gi_, 112 - 16 * j:240 - 16 * j]), R_(VL[:, gi_, 0:128 - 16 * j]),
                       j == 0, j == 7, D(ZB, VL), [db2])
                stt("dve", OUTW[:, gi_, 0:128], ident[:], R_(dvec[:, g:g + 1]), ps[:, b2, 0:128], ALU.mult, ALU.add,
                    [db2, d_const] + D(dvec), D(OUTW))
                yield
            fw.dma("sp", s5w[g0:g0 + GB].rearrange("g p n -> p g n"), R_(OUTW), reads=D(OUTW), writes=[d_s5w])
            yield

        dead_all = [d for d in order if d not in D(er, ei)]
        o[0] = PR.offset % PR.ap[0][0]
        TG = 16
        tr_ = alloc(TG * 72).rearrange("p (g n) -> p g n", n=72)
        ti_ = alloc(TG * 72).rearrange("p (g n) -> p g n", n=72)
        TB = alloc(TG * 2 * C).rearrange("p (g a c) -> p g a c", a=2, c=C)
        fw.op("dve", lambda e: e.tensor_scalar(out=small[:, 20:21], in0=small[:, 0:1], scalar1=0.0, scalar2=None, op0=ALU.mult),
              reads=[d_small], writes=dead_all + D(tr_, ti_, TB) + [d_small])
        yield
        for g0 in range(0, 32, TG):
            gsl = slice(g0, g0 + TG)
            TR = TB[:, :, 0, :]
            TI = TB[:, :, 1, :]
            CP_(TR[:, :, 0:1], er[:, gsl].unsqueeze(2))
            CP_(TI[:, :, 0:1], ei[:, gsl].unsqueeze(2))
            n = 1
            while n < 128:
                pr_ = TR[:, :, n - 1:n].to_broadcast([128, TG, n])
                pi_ = TI[:, :, n - 1:n].to_broadcast([128, TG, n])
                T2(tr_[:, :, 0:n], TR[:, :, 0:n], pr_, ALU.mult)
                T2(ti_[:, :, 0:n], TI[:, :, 0:n], pi_, ALU.mult)
                yield
                T2(TR[:, :, n:2 * n], tr_[:, :, 0:n], ti_[:, :, 0:n], ALU.subtract)
                T2(tr_[:, :, 0:n], TR[:, :, 0:n], pi_, ALU.mult)
                yield
                T2(ti_[:, :, 0:n], TI[:, :, 0:n], pr_, ALU.mult)
                T2(TI[:, :, n:2 * n], tr_[:, :, 0:n], ti_[:, :, 0:n], ALU.add)
                n *= 2
                yield
            CP_(TR[:, :, 128:136], TR[:, :, 0:1].to_broadcast([128, TG, 8]))
            CP_(TI[:, :, 128:136], TI[:, :, 0:1].to_broadcast([128, TG, 8]))
            fw.dma("sp", s5t[g0:g0 + TG].rearrange("g p n -> p g n"), R_(TB.rearrange("p g a c -> p g (a c)")), reads=D(TB), writes=[d_s5t])
            yield
        allv = list(order)
        fw.op("dve", lambda e: e.tensor_scalar(out=small[:, 20:21], in0=small[:, 0:1], scalar1=0.0, scalar2=None, op0=ALU.mult),
              reads=[d_small], writes=list(d_R2) + allv + [d_small])
        yield

    PUMP = int(os.environ.get("KPUMP", "1"))
    PRO = {"gen": None}

    def pump(n):
        g_ = PRO["gen"]
        if g_ is None:
            return
        for _ in range(n):
            try:
                next(g_)
            except StopIteration:
                PRO["gen"] = None
                return

    def mixer(t):
        rmsnorm(1)
        uT = R1f[:, 0:4, :]
        pooled = R1[:, 4:8, :]
        yT = R1[:, 0:4, :]
        Uall = R1[:, 0:4, :].rearrange("p k n -> p (k n)").rearrange("p (g c) -> p g c", c=C)
        Yall = R2f[:, 4:8, :].rearrange("p k n -> p (k n)").rearrange("p (g c) -> p g c", c=C)
        YallT = R2[:, 4:8, :].rearrange("p k n -> p (k n)").rearrange("p (g c) -> p g c", c=C)
        win_v = I["w_in"].rearrange("(k p) n -> p k n", p=128)

        for oc in (4, 5, 6, 7, 0, 1, 2, 3):
            wv, dw = wload(win_v[:, :, oc * 128:(oc + 1) * 128], [NK, 128])
            for s in range(NSUB):
                sl = slice(s * SUB, (s + 1) * SUB)
                b, db = bank()
                for k in range(NK):
                    mm(ps[:, b, 0:SUB], wv[:, k, :], hn[:, k, sl], k == 0, k == NK - 1, [dw, d_hn], [db])
                if oc < 4:
                    copy(alt(), R1[:, oc, sl], ps[:, b, 0:SUB], [db], [d_R1[oc]])
                else:
                    wg = oc - 4
                    w = WINS[wg]
                    copy(alt(), Ew(wg, True)[:, w - 1 + 2 * s:w + 1 + 2 * s, :], ps[:, b, 0:SUB].rearrange("p (r c) -> p r c", c=C),
                         [db], d_R2)
        for ch in range(4):
            fw.dma("sp", u_scr[t, ch * 128:(ch + 1) * 128, :], uT[:, ch, :], reads=[d_R1[ch]], writes=[d_uscr[t]])

        for j in range(8):
            src = u_scr[t].rearrange("(g h) (j c) -> j h g c", h=16, c=C)[j]
            fw.dma("pool", Uall[j * 16:(j + 1) * 16, :, :], src.bitcast(F32R), reads=[d_uscr[t]], writes=d_R1[0:4])

        for hb in range(2):
            sth = TMP[:, (hb % 2) * 128:(hb % 2) * 128 + 128]
            dsth = tdep("stg0")
            fw.dma("sp", sth[:, 0:64], I["st_re"][8 * t + 4 * hb:8 * t + 4 * hb + 4].rearrange("s g p -> (s g) p"), writes=[dsth])
            fw.dma("sp", sth[:, 64:128], I["st_im"][8 * t + 4 * hb:8 * t + 4 * hb + 4].rearrange("s g p -> (s g) p"), writes=[dsth])
            b, db = bank()
            tr(ps[:, b, 0:128], sth, ident[:], [dsth, d_const], [db])
            copy("dve", h0s[:, 4 * hb:4 * hb + 4, :], ps[:, b, 0:128].rearrange("p (s g) -> p s g", g=32), [db], [d_h0s])

        stp = TMP[:, 2048:3072]
        dstp = [tdep(f"stgs{j}") for j in range(8)]
        dstq = [tdep(f"stp{i}") for i in range(15)]
        ts("dve", TMP[:, 2048:2560], pcar[:, :, :].rearrange("p a n -> p (a n)"), 0.0, None, ALU.mult, None, [d_pcar], dstp + dstq)
        for i in range(15):
            fw.dma("sp", stp[i * 8:(i + 1) * 8, 0:512], I["st_pool"][8 * t:8 * t + 8, i, :], writes=[dstq[i]])
        b, db = bank()
        for wg in range(4):
            tr(ps[:, b, wg * 128:(wg + 1) * 128], stp[:, wg * 128:(wg + 1) * 128], ident[:], dstp + dstq + [d_const], [db])
        copy("dve", spT[:, :, :], ps[:, b, :].rearrange("p (a n) -> p a n", n=128), [db], [d_spT])
        for cs in range(8):
            out_toks.append(fw.dma("sp", O["o_spool"][8 * t + cs, 0:7, :], I["st_pool"][8 * t + cs, 8:15, :], reads=[tdep("d2d")]))
        for wg in range(4):
            w = WINS[wg]
            Ev = Ew(wg)
            Et = Ew(wg, True)
            copy(alt(), Et[:, 0:w - 1, 0], pcar[:, wg, 16 - w:15], d_R2 + [d_pcar], d_R2)
            for r in range(w - 1):
                copy(alt(), Et[:, r, CP:C], spT[:, wg, (16 - w + r) * 8:(17 - w + r) * 8], d_R2 + [d_spT], d_R2)
            if w == 16:
                copy(alt(), Et[:, 7:15, 1:CP], Ev[:, 15:23, 0:CP - 1], d_R2, d_R2)
                copy(alt(), Et[:, 0:7, 1:CP], Ev[:, 8:15, 0:CP - 1], d_R2, d_R2)
            else:
                copy(alt(), Et[:, 0:w - 1, 1:CP], Ev[:, 8:w + 7, 0:CP - 1], d_R2, d_R2)
        upT = TMP[:, 3072:3584].rearrange("p (a n) -> p a n", n=128)
        dup = tdep("xsc")
        for wg in range(4):
            w = WINS[wg]
            Ev = Ew(wg)
            copy(alt(), pcar[:, wg, 0:7], Ev[:, w:w + 7, CP - 2], d_R2, [d_pcar])
            copy(alt(), pcar[:, wg, 7:15], Ev[:, w - 1:w + 7, CP - 1], d_R2, [d_pcar])
            for j in range(8):
                copy(alt(), upT[:, wg, j * 8:(j + 1) * 8], Ev[:, w - 1 + j, CP:C], d_R2 + [d_zpad], [dup])
        b, db = bank()
        for wg in range(4):
            tr(ps[:, b, wg * 128:(wg + 1) * 128], upT[:, wg, :], ident[:], [dup, d_const], [db])
        ots = TMP[:, 0:512]
        dots = tdep("stg0")
        copy("dve", ots, ps[:, b, :], [db], [dots])
        for j in range(8):
            out_toks.append(fw.dma("sp", O["o_spool"][8 * t:8 * t + 8, 7 + j, :], ots[j * 8:(j + 1) * 8, :], reads=[dots]))
        if t == 1:
            b, db = bank()
            for wg in range(4):
                tr(ps[:, b, wg * 128:(wg + 1) * 128], pcar[:, wg, :], ident[:], [d_pcar, d_const], [db])
            otp = TMP[:, 512:1024]
            dotp = tdep("stg1")
            copy("dve", otp, ps[:, b, :], [db], [dotp])
            out_toks.append(fw.dma("sp", O["o_ppool"][:, :], otp[0:15, :], reads=[dotp]))

        Tt = TMP[:, 1024:1024 + 22 * C].rearrange("p (r c) -> p r c", c=C)
        dT = [tdep("stg1"), tdep("stgs0"), tdep("stgs1"), tdep("stgs2"), tdep("stgs3"), tdep("stgs4"), tdep("stgs5"),
              tdep("stgs6"), tdep("stgs7"), tdep("xsc")]
        for wg in range(4):
            w = WINS[wg]
            Ev = Ew(wg)
            R = 8 + w - 1
            tt("dve", Tt[:, 0:R - 1, :], Ev[:, 0:R - 1, :], Ev[:, 1:R, :], ALU.add, d_R2, dT)
            n = R - 1
            sh = 2
            while sh < w:
                tt("dve", Tt[:, 0:n - sh, :], Tt[:, 0:n - sh, :], Tt[:, sh:n, :], ALU.add, dT, dT)
                n -= sh
                sh *= 2
            stt("dve", pooled[:, wg, :].rearrange("p (r c) -> p r c", c=C), Tt[:, 0:8, :], 1.0 / w, Ev[:, w - 1:w + 7, :],
                ALU.mult, ALU.subtract, dT + d_R2, [d_R1[4 + wg]])
            if t == 0:
                for tau in range(w - 1):
                    j, c0 = tau % 8, tau // 8
                    stt("dve", pooled[:, wg, j * C + c0:j * C + c0 + 1], Tt[:, j, c0:c0 + 1], 1.0 / (tau + 1),
                        Ev[:, w - 1 + j, c0:c0 + 1], ALU.mult, ALU.subtract, dT + d_R2, [d_R1[4 + wg]])

        S5A = TMP[:, 1024:1024 + 6 * C].rearrange("p (a g c) -> p a g c", a=3, c=C)
        dS5A = dT
        gfend = TMP[:, 2048:2048 + 288].rearrange("p (g c) -> p g c", c=9)
        tend = TMP[:, 2336:2336 + 576].rearrange("p (g a c) -> p g a c", a=2, c=9)
        d_gfend = tdep("stgs0")
        stq_all = [tdep(f"stgs{j}") for j in range(8)] + [tdep(f"stp{i}") for i in range(15)]
        for a_ in range(2):
            fw.dma("sp", tend[:, :, a_, :], s5t.rearrange("g p (a c) -> p g a c", a=2)[:, :, a_, CP - 1:C], reads=[d_s5t],
                   writes=[tdep(f"tend{a_}")] + (stq_all if a_ == 0 else []))
        d_tend = tdep("tend0")
        d_tend1 = tdep("tend1")
        pair = {}

        def s5_front(pi_):
            g = 2 * pi_
            par = pi_ % 2
            wsl0, dws0 = wload(s5w[g], [5, 128], reads=[d_s5w])
            wsl1, dws1 = wload(s5w[g + 1], [5, 128], reads=[d_s5w])
            tabv = tabr[:, par, :, :]
            fw.dma("sp", tabv, s5t[g:g + 2].rearrange("g p n -> p g n"), reads=[d_s5t], writes=[d_tab[par]])
            bS = 4 * par
            dbS = [d_bank[bS], d_bank[bS + 1]]
            dbS2 = [d_bank[bS + 2], d_bank[bS + 3]]
            wsl = (wsl0, wsl1)
            dws = (dws0, dws1)
            for gg in range(2):
                U = Uall[:, g + gg, :]
                mm(ps[:, bS + gg, 0:C], wsl[gg][:, 1, :], U, True, True, [dws[gg], d_R1[g // 8]], [dbS[gg]])
                mm(ps[:, bS + 2 + gg, 0:C], wsl[gg][:, 2, :], U, True, True, [dws[gg], d_R1[g // 8]], [dbS2[gg]])
            pair[pi_] = (wsl, dws, tabv, bS, dbS, dbS2)

        s5_front(0)
        pend = []
        for pi_ in range(16):
            g = 2 * pi_
            par = pi_ % 2
            wsl, dws, tabv, bS, dbS, dbS2 = pair.pop(pi_)
            cosT = tabv[:, :, 0:C]
            sinT = tabv[:, :, C:2 * C]
            dU = d_R1[0:4]
            t1 = S5A[:, 0, :, :]
            t2 = S5A[:, 1, :, :]
            sp_ = S5A[:, 2, :, :]
            gf = s5tmp[:, par, :, :]
            dG = tdep(f"s5g{par}")
            X1 = xx[:, par, 0, :, :]
            X2 = xx[:, par, 1, :, :]
            tt("dve", t1, ps[:, bS:bS + 2, 0:C], cosT, ALU.mult, dbS + [d_tab[par]], dS5A)
            tt("dve", t2, ps[:, bS + 2:bS + 4, 0:C], sinT, ALU.mult, dbS2 + [d_tab[par]], dS5A)
            if pi_ + 1 < 16:
                s5_front(pi_ + 1)
            tt("dve", sp_, t1, t2, ALU.add, dS5A, dS5A)
            for gg in range(2):
                mcol = s5c[:, MV, g + gg:g + gg + 1]
                fw.op("dve", (lambda o_, d0, d1, ini: lambda e: e.tensor_tensor_scan(
                    out=o_, data0=d0, data1=d1, initial=ini, op0=ALU.mult, op1=ALU.add))(
                    gf[:, gg, 0:CP], mcol.to_broadcast([128, CP]), sp_[:, gg, 0:CP], s5c[:, H0, g + gg:g + gg + 1]),
                    reads=dS5A + [d_s5c], writes=[dG])
                stt("dve", gf[:, gg, CP:C], h0s[:, :, g + gg], mcol, sp_[:, gg, CP:C], ALU.mult, ALU.add,
                    dS5A + [d_h0s, d_s5c], [dG])
            tt("dve", X1[:, :, 1:CP], cosT[:, :, 0:CP - 1], gf[:, :, 0:CP - 1], ALU.mult, [dG, d_tab[par]], [d_xx[par]])
            tt("dve", X2[:, :, 1:CP], sinT[:, :, 0:CP - 1], gf[:, :, 0:CP - 1], ALU.mult, [dG, d_tab[par]], [d_xx[par]])
            copy("act", X1[:, :, 0:1], s5c[:, H0, g:g + 2].unsqueeze(2), [d_s5c], [d_xx[par]])
            for gg in range(2):
                copy("act", X1[:, gg, CP:C], h0s[:, :, g + gg], [d_h0s], [d_xx[par]])
            for gg in range(2):
                U = Uall[:, g + gg, :]
                mm(ps[:, bS + gg, 0:C], wsl[gg][:, 0, :], U, True, False, [dws[gg], d_R1[g // 8]], [dbS[gg]])
                mm(ps[:, bS + gg, 0:C], wsl[gg][:, 3, :], X1[:, gg, :], False, False, [dws[gg], d_xx[par]], [dbS[gg]])
                mm(ps[:, bS + gg, 0:C], wsl[gg][:, 4, :], X2[:, gg, :], False, True, [dws[gg], d_xx[par]], [dbS[gg]])
            copy("dve", gfend[:, g:g + 2, :], gf[:, :, CP - 1:C], [dG], [d_gfend] + (stq_all if pi_ == 0 else []))
            act(YallT[:, g:g + 2, :], ps[:, bS:bS + 2, 0:C], AF.Gelu_apprx_tanh, dbS,
                [d_yall[pi_ // 4]] + (d_R2[4:8] if pi_ == 0 else []))
            if pi_ % 4 == 3:
                ch = pi_ // 4
                for k in range(8):
                    dst = y_scr[t].rearrange("(g q) (k c) -> k q g c", q=16, c=C)[k][:, 8 * ch:8 * ch + 8, :]
                    fw.dma("act", dst, Yall[k * 16:(k + 1) * 16, 8 * ch:8 * ch + 8, :], reads=[d_yall[ch]], writes=[d_yq[t][ch]])
                pend.append((pi_ + 2, ch))
            while pend and (pend[0][0] <= pi_ or pi_ == 15):
                _, ch_ = pend.pop(0)
                fw.dma("pool", yT[:, ch_, :], y_scr[t, ch_ * 128:(ch_ + 1) * 128, :].bitcast(F32R), reads=[d_yq[t][ch_]], writes=[d_R1[ch_]])
        tt("dve", s5f[:, 0, :], tend[:, :, 0, 0], gfend[:, :, 0], ALU.mult, [d_tend, d_tend1, d_gfend], [d_s5f])
        tt("dve", s5f[:, 1, :], tend[:, :, 1, 0], gfend[:, :, 0], ALU.mult, [d_tend, d_tend1, d_gfend], [d_s5f])
        tt("dve", fs[:, 0, :, :].rearrange("p s g -> p g s"), tend[:, :, 0, 1:9], gfend[:, :, 1:9], ALU.mult, [d_tend, d_tend1, d_gfend], [d_fs])
        tt("dve", fs[:, 1, :, :].rearrange("p s g -> p g s"), tend[:, :, 1, 1:9], gfend[:, :, 1:9], ALU.mult, [d_tend, d_tend1, d_gfend], [d_fs])
        ts("dve", small[:, 21:22], small[:, 0:1], 0.0, None, ALU.mult, None, [d_small], d_yall + d_R2[4:8] + [d_small])
        b, db = bank()
        mm(ps[:, b, 0:32], ident[:], s5f[:, 0, :], True, False, [d_s5f, d_const], [db])
        mm(ps[:, b, 0:32], jt[:], s5f[:, 1, :], False, True, [d_s5f, d_jt], [db])
        copy("dve", s5c[:, H0, :], ps[:, b, 0:32], [db], [d_s5c])
        b2, db2 = bank()
        mm(ps[:, b2, 0:256], ident[:], fs[:, 0, :, :].rearrange("p s g -> p (s g)"), True, False, [d_fs, d_const], [db2])
        mm(ps[:, b2, 0:256], jt[:], fs[:, 1, :, :].rearrange("p s g -> p (s g)"), False, True, [d_fs, d_jt], [db2])
        hsf = TMP[:, 0:256]
        dhs = tdep("stg0")
        copy("dve", hsf, ps[:, b2, 0:256], [db2], [dhs])
        b3, db3 = bank()
        tr(ps[:, b3, 0:128], hsf[:, 0:128], ident[:], [dhs, d_const], [db3])
        tr(ps[:, b3, 128:256], hsf[:, 128:256], ident[:], [dhs, d_const], [db3])
        hso = TMP[:, 256:512]
        copy("dve", hso, ps[:, b3, 0:256], [db3], [dhs])
        for cs in range(8):
            blk, r0 = cs // 4, (cs % 4) * 32
            out_toks.append(fw.dma("sp", O["o_sre"][8 * t + cs], hso[r0:r0 + 32, blk * 128:blk * 128 + 64], reads=[dhs]))
            out_toks.append(fw.dma("sp", O["o_sim"][8 * t + cs], hso[r0:r0 + 32, blk * 128 + 64:blk * 128 + 128], reads=[dhs]))
        if t == 1:
            hp = TMP[:, 512:640]
            dhp = tdep("stg1")
            memset("pool", hp, 0.0, [dhp])
            copy("dve", hp[:, 0:32], s5c[:, H0, :], [d_s5c], [dhp])
            b4, db4 = bank()
            tr(ps[:, b4, 0:128], hp, ident[:], [dhp, d_const], [db4])
            hpo = TMP[:, 640:768]
            copy("dve", hpo, ps[:, b4, 0:128], [db4], [dhp])
            out_toks.append(fw.dma("sp", O["o_pre"][:, :], hpo[0:32, 0:64], reads=[dhp]))
            out_toks.append(fw.dma("sp", O["o_pim"][:, :], hpo[0:32, 64:128], reads=[dhp]))

        glu_v = I["w_glu"].rearrange("(k p) n -> p k n", p=128)
        MT = TMP[:, 0:3 * SUB].rearrange("p (a n) -> p a n", n=SUB)
        dMT = [tdep("stg0"), tdep("stg1")]
        for m in range(NK):
            wgs, dgs = wload(win_v[:, :, 1024 + m * 128:1024 + (m + 1) * 128], [NK, 128])
            wgp, dgp = wload(win_v[:, :, 2048 + m * 128:2048 + (m + 1) * 128], [NK, 128])
            wa, da = wload(glu_v[:, :, m * 128:(m + 1) * 128], [4, 128])
            wb_, dwb = wload(glu_v[:, :, 1024 + m * 128:1024 + (m + 1) * 128], [4, 128])
            wgm = m // 2
            wp, dwp = wload(I["w_pool"][wgm, :, (m % 2) * 128:(m % 2 + 1) * 128], [128])
            for s in range(NSUB):
                sl = slice(s * SUB, (s + 1) * SUB)
                bA, dA_ = bank()
                bB, dB_ = bank()
                bS, dSg = bank()
                bP, dPg = bank()
                bY, dY_ = bank()
                for k in range(NK):
                    mm(ps[:, bS, 0:SUB], wgs[:, k, :], hn[:, k, sl], k == 0, k == NK - 1, [dgs, d_hn], [dSg])
                for k in range(NK):
                    mm(ps[:, bP, 0:SUB], wgp[:, k, :], hn[:, k, sl], k == 0, k == NK - 1, [dgp, d_hn], [dPg])
                mm(ps[:, bY, 0:SUB], wp, pooled[:, wgm, sl], True, True, [dwp, d_R1[4 + wgm]], [dY_])
                for k in range(4):
                    mm(ps[:, bB, 0:SUB], wb_[:, k, :], yT[:, k, sl], k == 0, k == 3, [dwb, d_R1[k]], [dB_])
                for k in range(4):
                    mm(ps[:, bA, 0:SUB], wa[:, k, :], yT[:, k, sl], k == 0, k == 3, [da, d_R1[k]], [dA_])
                act(MT[:, 0, :], ps[:, bB, 0:SUB], AF.Sigmoid, [dB_], dMT)
                act(MT[:, 1, :], ps[:, bS, 0:SUB], AF.Sigmoid, [dSg], dMT)
                act(MT[:, 2, :], ps[:, bP, 0:SUB], AF.Sigmoid, [dPg], dMT)
                tt("dve", MT[:, 0, :], ps[:, bA, 0:SUB], MT[:, 0, :], ALU.mult, [dA_] + dMT, dMT)
                tt("dve", MT[:, 0, :], MT[:, 0, :], MT[:, 1, :], ALU.mult, dMT, dMT)
                stt("dve", MT[:, 2, :], ps[:, bY, 0:SUB], pscale[:, m:m + 1], MT[:, 2, :], ALU.mult, ALU.mult, [dY_, d_psc] + dMT, dMT)
                tt("dve", R2[:, m, sl], MT[:, 0, :], MT[:, 2, :], ALU.add, dMT, [d_R2[m]])

        for k in range(NK):
            copy(alt(), hn[:, k, :], R2[:, k, :], [d_R2[k]], [d_hn])
        wo_v = I["w_out"].rearrange("(k p) n -> p k n", p=128)
        for m in range(NK):
            wo, dwo = wload(wo_v[:, :, m * 128:(m + 1) * 128], [NK, 128])
            for s in range(NSUB):
                sl = slice(s * SUB, (s + 1) * SUB)
                b, db = bank()
                for k in range(NK):
                    mm(ps[:, b, 0:SUB], wo[:, k, :], hn[:, k, sl], k == 0, k == NK - 1, [dwo, d_hn], [db])
                tt("dve", xT[:, m, sl], ps[:, b, 0:SUB], xT[:, m, sl], ALU.add, [db, d_xT[m]], [d_xT[m]])

    kst = os.environ.get("KSTAGE", "full")
    kparts = os.environ.get("KPARTS", "load,store").split(",")
    if kst in ("full", "mix", "pro"):
        PRO["gen"] = prologue()
        pump(9)
        if kst in ("pro", "mix"):
            pump(100000)
    zero_pads()
    for t in range(int(os.environ.get("KT", "2"))):
        if "load" in kparts:
            load_tile(t)
        if kst in ("full", "ffn1", "nomix"):
            ffn(0, I["ffn1_up"], I["ffn1_down"])
        if kst in ("full", "mix"):
            pump(100000)
            mixer(t)
        if kst in ("full", "nomix"):
            ffn(2, I["ffn2_up"], I["ffn2_down"])
        if "store" in kparts:
            final_store(t, t == 0)
    if kst == "pro":
        out_toks.extend([d_s5w.w, d_s5t.w])
    elif "store" not in kparts:
        out_toks.append(fw.dma("sp", O["yp"][0:128, :], xT[:, 0, 0:1024], reads=d_xT))

    fw.finish_wait("sp", out_toks)
    fw.emit()
    es.close()
    return nc


_CACHE = {}


def kernel(**inputs):
    f = lambda a: np.ascontiguousarray(np.asarray(a, dtype=np.float32))
    shared = {
        "ffn1_norm": f(inputs["ffn1_norm"][0]), "ffn1_up": f(inputs["ffn1_up"][0]), "ffn1_down": f(inputs["ffn1_down"][0]),
        "mix_norm": f(inputs["mix_norm"][0]), "w_in": f(inputs["w_in"][0]),
        "lam_re": f(inputs["lam_re"][0]), "lam_im": f(inputs["lam_im"][0]), "log_step": f(inputs["log_step"][0]),
        "b_re": f(inputs["b_re"][0]), "b_im": f(inputs["b_im"][0]), "c_re": f(inputs["c_re"][0]), "c_im": f(inputs["c_im"][0]),
        "d_skip": f(inputs["d_skip"][0]), "w_glu": f(inputs["w_glu"][0]), "w_pool": f(inputs["w_pool"][0]),
        "pool_scale": f(inputs["pool_scale"][0]), "w_out": f(inputs["w_out"][0]),
        "ffn2_norm": f(inputs["ffn2_norm"][0]), "ffn2_up": f(inputs["ffn2_up"][0]), "ffn2_down": f(inputs["ffn2_down"][0]),
        "final_norm": f(inputs["final_norm"]),
    }
    xp = f(inputs["x_prompt"])
    xs = f(inputs["x_sample"])
    sre = f(inputs["state_ssm_re"][0])
    sim = f(inputs["state_ssm_im"][0])
    spool = f(inputs["state_pool"][0])
    in_maps = []
    for c in range(8):
        m = dict(shared)
        m["xp"] = xp[c]
        m["xs"] = xs[16 * c:16 * c + 16]
        m["st_re"] = sre[16 * c:16 * c + 16]
        m["st_im"] = sim[16 * c:16 * c + 16]
        m["st_pool"] = spool[16 * c:16 * c + 16]
        in_maps.append(m)
    if "nc" not in _CACHE:
        _CACHE["nc"] = build_program()
    res = run_bass_kernel_spmd(_CACHE["nc"], in_maps, core_ids=list(range(8)))
    r = res.results
    y_prompt = np.stack([r[c]["yp"] for c in range(8)])
    y_sample = np.concatenate([r[c]["ys"] for c in range(8)], axis=0)
    p_re = np.stack([r[c]["o_pre"] for c in range(8)])[None]
    p_im = np.stack([r[c]["o_pim"] for c in range(8)])[None]
    p_pool = np.stack([r[c]["o_ppool"] for c in range(8)])[None]
    s_re = np.concatenate([r[c]["o_sre"] for c in range(8)], axis=0)[None]
    s_im = np.concatenate([r[c]["o_sim"] for c in range(8)], axis=0)[None]
    s_pool = np.concatenate([r[c]["o_spool"] for c in range(8)], axis=0)[None]
    return (y_prompt, y_sample, p_re, p_im, p_pool, s_re, s_im, s_pool)
```
